# Optimizing a Trainium2 kernel written in Bass

```python
import math
import jax, jax.numpy as jnp
from jax import lax
import numpy as np

D_MODEL = 1024
BATCH = 4
SEQ = 8192
DEPTH = 1

ATT_HEADS = 8
ATT_QK_DIM = 64
ATT_V_DIM = 2 * ATT_QK_DIM
ATT_QK_WIDTH = ATT_HEADS * 2 * ATT_QK_DIM
ATT_V_WIDTH = ATT_HEADS * ATT_V_DIM
REC_HEADS = 8
REC_K_DIM = 128
REC_V_DIM = 128
REC_F_WIDTH = REC_HEADS * REC_K_DIM
REC_V_WIDTH = REC_HEADS * REC_V_DIM
REC_CHUNK = 64
D_FF = 4 * D_MODEL
ROPE_THETA = 500000.0
ROPE_DIM = ATT_QK_DIM // 4
Q_BLOCK = 128
EPS = 1e-6
N_ADA = 6
IN_WIDTHS = (ATT_QK_WIDTH, ATT_QK_WIDTH, ATT_V_WIDTH,
             REC_F_WIDTH, REC_F_WIDTH, REC_V_WIDTH, REC_V_WIDTH,
             D_MODEL, D_MODEL)
IN_WIDTH = sum(IN_WIDTHS)

kernel_name = "hybrid_diffattn_hgrn2_gated_merge"


def rmsnorm(x, w):
    xf = x.astype(jnp.float32)
    y = xf * lax.rsqrt(jnp.mean(xf * xf, axis=-1, keepdims=True) + EPS)
    return (y * w.astype(jnp.float32)).astype(x.dtype)


def lambda_init_fn(layer_idx):
    return 0.8 - 0.6 * math.exp(-0.3 * layer_idx)


def rope_partial(t, cos, sin):
    half = ROPE_DIM // 2
    t1, t2, rest = t[..., :half], t[..., half:ROPE_DIM], t[..., ROPE_DIM:]
    cos = cos.astype(t.dtype)
    sin = sin.astype(t.dtype)
    return jnp.concatenate([t1 * cos - t2 * sin, t2 * cos + t1 * sin, rest], axis=-1)


def diff_attention(q, k, v, lam):
    B, S = q.shape[0], q.shape[1]
    nb = S // Q_BLOCK
    qb = q.reshape(B, nb, Q_BLOCK, ATT_HEADS, 2, ATT_QK_DIM).transpose(1, 0, 2, 3, 4, 5)
    key_idx = jnp.arange(S)
    scale = ATT_QK_DIM ** -0.5

    def block(args):
        qi, i = args
        s = jnp.einsum('bqhmd,bkhmd->bhmqk', qi, k).astype(jnp.float32) * scale
        q_idx = i * Q_BLOCK + jnp.arange(Q_BLOCK)
        mask = key_idx[None, :] <= q_idx[:, None]
        p = jax.nn.softmax(jnp.where(mask, s, -jnp.inf), axis=-1)
        a = (p[:, :, 0] - lam * p[:, :, 1]).astype(v.dtype)
        return jnp.einsum('bhqk,bkhe->bqhe', a, v)

    o = lax.map(block, (qb, jnp.arange(nb)))
    return o.transpose(1, 0, 2, 3, 4).reshape(B, S, ATT_HEADS, ATT_V_DIM)


def hgrn2_chunkwise(q, k, v, log_f):
    B, S = q.shape[0], q.shape[1]
    nc = S // REC_CHUNK

    def to_chunks(t):
        return t.astype(jnp.float32).reshape(B, nc, REC_CHUNK, REC_HEADS, t.shape[-1]).transpose(1, 0, 3, 2, 4)

    causal = jnp.tril(jnp.ones((REC_CHUNK, REC_CHUNK), dtype=bool))

    def step(state, inp):
        qc, kc, vc, gc = inp
        b = jnp.cumsum(gc, axis=2)
        rel = jnp.where(causal[:, :, None], b[:, :, :, None, :] - b[:, :, None, :, :], -jnp.inf)
        scores = jnp.einsum('bhtk,bhsk,bhtsk->bhts', qc, kc, jnp.exp(rel))
        o = (jnp.einsum('bhts,bhsv->bhtv', scores, vc)
             + jnp.einsum('bhtk,bhkv->bhtv', qc * jnp.exp(b), state))
        b_last = b[:, :, -1:, :]
        state = (jnp.exp(b_last[:, :, 0, :, None]) * state
                 + jnp.einsum('bhsk,bhsv->bhkv', kc * jnp.exp(b_last - b), vc))
        return state, o

    s0 = jnp.zeros((B, REC_HEADS, REC_K_DIM, REC_V_DIM), jnp.float32)
    _, o = lax.scan(step, s0, (to_chunks(q), to_chunks(k), to_chunks(v), to_chunks(log_f)))
    return o.transpose(1, 0, 3, 2, 4).reshape(B, S, REC_HEADS, REC_V_DIM)


def setup_inputs(seed: int = 0) -> dict:
    key = jax.random.key(seed)
    ks = jax.random.split(key, 24)
    nrm = jax.random.normal
    f32 = jnp.float32

    def gain(k, shape):
        return 1.0 + 0.02 * nrm(k, shape, f32)

    offsets = jax.random.randint(ks[2], (BATCH, 1), 0, 4096, dtype=jnp.int32)
    positions = offsets + jnp.arange(SEQ, dtype=jnp.int32)[None, :]
    return {
        "x": nrm(ks[0], (BATCH, SEQ, D_MODEL), f32),
        "c": nrm(ks[1], (BATCH, D_MODEL), f32),
        "positions": positions,
        "w_ada": nrm(ks[3], (DEPTH, D_MODEL, N_ADA * D_MODEL), f32) * D_MODEL ** -0.5,
        "b_ada": 0.02 * nrm(ks[4], (DEPTH, N_ADA * D_MODEL), f32),
        "norm_mix": gain(ks[5], (DEPTH, D_MODEL)),
        "w_in": nrm(ks[6], (DEPTH, D_MODEL, IN_WIDTH), f32) * D_MODEL ** -0.5,
        "lam_q1": 0.1 * nrm(ks[7], (DEPTH, ATT_QK_DIM), f32),
        "lam_k1": 0.1 * nrm(ks[8], (DEPTH, ATT_QK_DIM), f32),
        "lam_q2": 0.1 * nrm(ks[9], (DEPTH, ATT_QK_DIM), f32),
        "lam_k2": 0.1 * nrm(ks[10], (DEPTH, ATT_QK_DIM), f32),
        "subln_w": gain(ks[11], (DEPTH, ATT_V_DIM)),
        "lb_logits": 0.5 * nrm(ks[12], (DEPTH + 1, REC_F_WIDTH), f32),
        "rec_norm_w": gain(ks[13], (DEPTH, REC_V_DIM)),
        "w_proj_att": nrm(ks[14], (DEPTH, ATT_V_WIDTH, D_MODEL), f32) * ATT_V_WIDTH ** -0.5,
        "w_proj_rec": nrm(ks[15], (DEPTH, REC_V_WIDTH, D_MODEL), f32) * REC_V_WIDTH ** -0.5,
        "w_out": nrm(ks[16], (DEPTH, D_MODEL, D_MODEL), f32) * D_MODEL ** -0.5,
        "norm_mlp": gain(ks[17], (DEPTH, D_MODEL)),
        "w_mlp_in": nrm(ks[18], (DEPTH, D_MODEL, D_FF), f32) * D_MODEL ** -0.5,
        "w_mlp_out": nrm(ks[19], (DEPTH, D_FF, D_MODEL), f32) * D_FF ** -0.5,
        "norm_final": gain(ks[20], (D_MODEL,)),
    }


def reference(x, c, positions, w_ada, b_ada, norm_mix, w_in, lam_q1, lam_k1, lam_q2, lam_k2,
              subln_w, lb_logits, rec_norm_w, w_proj_att, w_proj_rec, w_out, norm_mlp,
              w_mlp_in, w_mlp_out, norm_final):
    B, S, _ = x.shape
    f32 = jnp.float32
    split_points = np.cumsum(IN_WIDTHS)[:-1].tolist()

    inv_freq = ROPE_THETA ** (-jnp.arange(0, ROPE_DIM, 2, dtype=f32) / ROPE_DIM)
    ang = positions.astype(f32)[..., None] * inv_freq
    cos = jnp.cos(ang)[:, :, None, None, :]
    sin = jnp.sin(ang)[:, :, None, None, :]

    lb_p = jax.nn.softmax(lb_logits.astype(f32), axis=0)
    lb_cum = jnp.cumsum(lb_p, axis=0) - lb_p[0]

    cond = jax.nn.silu(c)
    for l in range(DEPTH):
        ada = cond @ w_ada[l] + b_ada[l]
        sh_m, sc_m, g_m, sh_f, sc_f, g_f = [a[:, None, :] for a in jnp.split(ada, N_ADA, axis=-1)]

        h = rmsnorm(x, norm_mix[l]) * (1.0 + sc_m) + sh_m
        proj = h @ w_in[l]
        q, k, v, rq, rf, ri, rg, ga, gr = jnp.split(proj, split_points, axis=-1)

        lam_init = lambda_init_fn(l)
        lam = (jnp.exp(jnp.sum(lam_q1[l].astype(f32) * lam_k1[l].astype(f32)))
               - jnp.exp(jnp.sum(lam_q2[l].astype(f32) * lam_k2[l].astype(f32))) + lam_init)
        qa = rope_partial(q.reshape(B, S, ATT_HEADS, 2, ATT_QK_DIM), cos, sin)
        ka = rope_partial(k.reshape(B, S, ATT_HEADS, 2, ATT_QK_DIM), cos, sin)
        va = v.reshape(B, S, ATT_HEADS, ATT_V_DIM)
        o_a = diff_attention(qa, ka, va, lam)
        o_a = (rmsnorm(o_a, subln_w[l]) * (1.0 - lam_init)).reshape(B, S, ATT_V_WIDTH)

        lb = lb_cum[l + 1]
        f = lb + (1.0 - lb) * jax.nn.sigmoid(rf.astype(f32))
        heads_k = lambda t: t.reshape(B, S, REC_HEADS, REC_K_DIM)
        o_r = hgrn2_chunkwise(heads_k(rq), heads_k(1.0 - f), ri.reshape(B, S, REC_HEADS, REC_V_DIM),
                              heads_k(jnp.log(f)))
        o_r = o_r.astype(x.dtype)
        o_r = (rmsnorm(o_r, rec_norm_w[l]) * jax.nn.silu(rg.reshape(B, S, REC_HEADS, REC_V_DIM))
               ).reshape(B, S, REC_V_WIDTH)

        y = jax.nn.sigmoid(ga) * (o_a @ w_proj_att[l]) + jax.nn.sigmoid(gr) * (o_r @ w_proj_rec[l])
        x = x + g_m * (y @ w_out[l])

        h = rmsnorm(x, norm_mlp[l]) * (1.0 + sc_f) + sh_f
        u = jnp.square(jax.nn.relu(h @ w_mlp_in[l]))
        x = x + g_f * (u @ w_mlp_out[l])

    return rmsnorm(x, norm_final)
```

```python
import numpy as np
import ml_dtypes
from contextlib import ExitStack
import concourse.bass as bass
import concourse.mybir as mybir
from concourse.bass_utils import run_bass_kernel_spmd

F32 = mybir.dt.float32
BF16 = mybir.dt.bfloat16
I32 = mybir.dt.int32
AF = mybir.ActivationFunctionType
ALU = mybir.AluOpType
AX = mybir.AxisListType

SEM_MAX = 30000
D = 1024
NEG = -30000.0


class Tl:
    __slots__ = ("name", "w", "r", "rd", "dram", "ws", "nd", "sem")

    def __init__(self, name, dram=False):
        self.name = name
        self.w = None
        self.r = {}
        self.rd = []
        self.dram = dram
        self.ws = []
        self.nd = 0
        self.sem = None


class Op:
    __slots__ = ("eng", "fn", "deps", "dma", "signal", "sem", "val", "inc", "anchor")

    def __init__(self, eng, fn, dma=False, anchor=None):
        self.eng = eng
        self.fn = fn
        self.deps = []
        self.dma = dma
        self.signal = dma
        self.sem = None
        self.val = 0
        self.inc = 1
        self.anchor = anchor


class Prog:
    ENGS = ("pe", "act", "dve", "pool", "sp")

    def __init__(self, nc):
        self.nc = nc
        self.ops = {e: [] for e in self.ENGS}

    def _dep(self, op, d):
        if d is None or d is op:
            return
        if d.eng == "pe" and op.eng == "pe" and not d.dma and not op.dma:
            return
        op.deps.append(d)

    def op(self, eng, fn, r=(), w=(), dma=False, anchor=None):
        o = Op(eng, fn, dma, anchor)
        for t in r:
            if t.dram:
                for d in t.ws:
                    self._dep(o, d)
            else:
                self._dep(o, t.w)
        for t in w:
            if t.dram:
                continue
            self._dep(o, t.w)
            for d in t.r.values():
                self._dep(o, d)
            for d in t.rd:
                self._dep(o, d)
        for t in r:
            if t.dram:
                continue
            if dma:
                t.rd.append(o)
            else:
                t.r[eng] = o
        for t in w:
            if t.dram:
                t.ws.append(o)
            else:
                t.w = o
                t.r = {}
                t.rd = []
        self.ops[eng].append(o)
        return o

    def barrier(self):
        if not hasattr(self, "bar_idx"):
            self.bar_idx = {e: 0 for e in self.ENGS}
        lasts = []
        for e in self.ENGS:
            for o in reversed(self.ops[e]):
                if not o.dma:
                    lasts.append(o)
                    break
        dmas = [o for e in self.ENGS for o in self.ops[e][self.bar_idx[e]:] if o.dma]
        for e in self.ENGS:
            o = Op(e, lambda eng: eng.nop())
            for d in lasts + dmas:
                self._dep(o, d)
            self.ops[e].append(o)
            self.bar_idx[e] = len(self.ops[e])

    def dma(self, out_ap, in_ap, r, w, anchor, eng="sp"):
        return self.op(eng, lambda e: e.dma_start(out=out_ap, in_=in_ap), r=r, w=w, dma=True, anchor=anchor)

    def emit(self, stack):
        nc = self.nc
        for e in self.ENGS:
            for o in self.ops[e]:
                for d in o.deps:
                    d.signal = True
        engsems = {e: [] for e in self.ENGS}
        semrank = {}
        for e in self.ENGS:
            cnt = SEM_MAX
            cur = None
            for o in self.ops[e]:
                if o.dma:
                    a = o.anchor
                    if a.sem is None:
                        a.sem = stack.enter_context(nc.semaphore("d_" + a.name))
                    a.nd += 1
                    o.sem = a.sem
                    o.val = 16 * a.nd
                    o.inc = 16
                elif o.signal:
                    if cnt >= SEM_MAX:
                        cur = stack.enter_context(nc.semaphore("e_%s_%d" % (e, len(engsems[e]))))
                        semrank[id(cur)] = (e, len(engsems[e]))
                        engsems[e].append(cur)
                        cnt = 0
                    cnt += 1
                    o.sem = cur
                    o.val = cnt
        blk = stack.enter_context(nc.Block())
        engobj = {"pe": "tensor", "act": "scalar", "dve": "vector", "pool": "gpsimd", "sp": "sync"}

        def body(e):
            def run(eng):
                waited = {}
                for o in self.ops[e]:
                    need = {}
                    for d in o.deps:
                        k = id(d.sem)
                        if waited.get(k, 0) >= d.val:
                            continue
                        if k not in need or need[k][1] < d.val:
                            need[k] = (d.sem, d.val)
                    for k, (s, v) in need.items():
                        eng.wait_ge(s, v)
                        waited[k] = v
                        if k in semrank:
                            ee, rk = semrank[k]
                            for j in range(rk):
                                waited[id(engsems[ee][j])] = SEM_MAX + 1
                    ins = o.fn(eng)
                    if o.signal:
                        ins.then_inc(o.sem, o.inc)
            return run

        for e in self.ENGS:
            if self.ops[e]:
                getattr(blk, engobj[e])(body(e))


class Alloc:
    def __init__(self, nc, base=16384, limit=229000):
        self.nc = nc
        self.off = base
        self.limit = limit
        self.n = 0

    def tile(self, shape, dtype, name):
        esz = 2 if dtype == BF16 else 4
        nb = (esz * int(np.prod(shape[1:])) + 63) // 64 * 64
        off = self.off
        self.off += nb
        assert self.off <= self.limit, "SBUF overflow at %s: %d" % (name, self.off)
        self.n += 1
        return self.nc.alloc_sbuf_tensor_at("%s_%d" % (name, self.n), list(shape), dtype, offset=off)

    def mark(self):
        return self.off

    def release(self, m):
        self.off = m


CF = dict(identf=(0, 128), uincl=(128, 256), vaft=(256, 384), cind=(384, 388), invf=(388, 396), eps=(396, 397),
          one=(397, 398), rmask3=(398, 399), vaft128=(400, 528))
CH = 32
NCH = 128 // CH
NCF = 528
CB = dict(ident=(0, 128), caus=(128, 256), m01=(256, 768), onesm=(768, 896))
NCB = 896


def zipper(gens):
    gens = list(gens)
    while gens:
        for g in list(gens):
            try:
                next(g)
            except StopIteration:
                gens.remove(g)


def make_consts():
    cf = np.zeros((128, NCF), np.float32)
    s = np.arange(128)
    ch = s // CH
    same = ch[:, None] == ch[None, :]
    cf[:, 0:128] = np.eye(128)
    cf[:, 128:256] = (same & (s[:, None] <= s[None, :]))
    cf[:, 256:384] = (same & (s[:, None] > s[None, :]))
    for c_ in range(NCH):
        cf[:, 384 + c_] = (ch == c_)
    invf = (np.float32(500000.0) ** (-np.arange(0, 16, 2, dtype=np.float32) / np.float32(16))).astype(np.float32)
    cf[:, 388:396] = invf[None, :]
    cf[:, 396] = 1e-6
    cf[:, 397] = 1.0
    cf[:, 398] = (s >= 96)
    cf[:, 400:528] = (s[:, None] > s[None, :])
    cb = np.zeros((128, NCB), np.float32)
    cb[:, 0:128] = np.eye(128)
    cb[:, 128:256] = np.where(s[:, None] <= s[None, :], 0.0, NEG)
    m01 = (same & (s[:, None] <= s[None, :])).astype(np.float32)
    cb[:, 256:768] = np.tile(m01, (1, 4))
    cb[:, 768:896] = 1.0 / 128.0
    return cf, cb.astype(ml_dtypes.bfloat16)


def build(NBL, dbg=False):
    NOWN = (NBL - 1) // 2
    NTOK = NBL * 128
    NQ = NOWN * 128
    nc = bass.Bass("TRN2", target_bir_lowering=False)

    def din(name, shape, dt=F32):
        return nc.dram_tensor(name, list(shape), dt, kind="ExternalInput").ap()

    skind = "ExternalOutput" if dbg else "Internal"

    def dscr(name, shape, dt):
        return nc.dram_tensor(name, list(shape), dt, kind=skind).ap()

    xl = din("xl", [NTOK, D])
    posl = din("posl", [128, NBL], I32)
    cT = din("cT", [128, 8])
    flag_d = din("flag", [128, 1])
    kb0_d = din("kb0", [128, 1])
    w_ada = din("w_ada", [D, 6 * D])
    b_ada = din("b_ada", [1, 6 * D])
    norm_mix = din("norm_mix", [1, D])
    w_in = din("w_in", [D, 9 * D])
    lamv = din("lamv", [1, 256])
    subw_d = din("subln_w", [128, 1])
    lb_logits = din("lb_logits", [2, D])
    recw_d = din("rec_norm_w", [128, 1])
    w_pa = din("w_proj_att", [D, D])
    w_pr = din("w_proj_rec", [D, D])
    w_o = din("w_out", [D, D])
    norm_mlp = din("norm_mlp", [1, D])
    w_mi = din("w_mlp_in", [D, 4 * D])
    w_mo = din("w_mlp_out", [4 * D, D])
    norm_final = din("norm_final", [1, D])
    cf_d = din("cf", [128, NCF])
    cb_d = din("cb", [128, NCB], BF16)
    out = nc.dram_tensor("out", [NQ, D], F32, kind="ExternalOutput").ap()

    HT = dscr("HT", [NBL, 128, D], BF16)
    KT = dscr("KT", [8, 128, NTOK], BF16)
    Vd = dscr("Vd", [NTOK, D], BF16)
    QT = dscr("QT", [8, 128, NQ], BF16)
    ORT = dscr("ORT", [128, 8, NQ], BF16)
    SGA = dscr("SGA", [128, 8, NQ], BF16)
    SGR = dscr("SGR", [128, 8, NQ], BF16)
    X1 = dscr("X1", [NQ, D], F32)
    ROWS = dscr("ROWS", [8, D], F32)
    T_HT, T_KT, T_V, T_QT, T_ORT, T_SGA, T_SGR, T_X1, T_ROWS = [Tl(n, dram=True) for n in
                                                               ("HT", "KT", "Vd", "QT", "ORT", "SGA", "SGR", "X1", "ROWS")]
    T_out = Tl("out", dram=True)

    A = Alloc(nc)
    P = Prog(nc)
    ntl = [0]

    def T(name):
        ntl[0] += 1
        return Tl("%s%d" % (name, ntl[0]))

    def tile(shape, dt, name):
        return A.tile(shape, dt, name), T(name)

    cf, Tcf = tile([128, NCF], F32, "cf")
    cb, Tcb = tile([128, NCB], BF16, "cb")
    flag, Tflag = tile([128, 1], F32, "flag")
    kb0, Tkb0 = tile([128, 1], F32, "kb0")
    subw, Tsubw = tile([128, 1], F32, "subw")
    recw, Trecw = tile([128, 1], F32, "recw")
    neglam, Tnl = tile([128, 1], F32, "neglam")
    P.dma(cf[:], cf_d, [], [Tcf], Tcf)
    P.dma(cb[:], cb_d, [], [Tcb], Tcb)
    P.dma(flag[:], flag_d, [], [Tflag], Tflag)
    P.dma(kb0[:], kb0_d, [], [Tkb0], Tkb0)
    P.dma(subw[:], subw_d, [], [Tsubw], Tsubw)
    P.dma(recw[:], recw_d, [], [Trecw], Trecw)

    def cfs(k):
        a, b = CF[k]
        return cf[:, a:b]

    def cbs(k):
        a, b = CB[k]
        return cb[:, a:b]

    eps_ap = cfs("eps")
    one_ap = cfs("one")

    PS_all = nc.alloc_psum_tensor("psall", [128, 4096], F32)
    PSB_all = PS_all.bitcast(BF16)
    PS = [PS_all[:, i * 512:(i + 1) * 512] for i in range(8)]
    PSB = [PSB_all[:, i * 1024:(i + 1) * 1024] for i in range(8)]
    bank = [0]
    bank_of = {}
    last_bank = [0]

    def psum(shape, dt, name):
        b = bank[0]
        bank[0] += 1
        assert b < 8, "out of PSUM banks at " + name
        bank_of[name + str(b)] = b
        last_bank[0] = b
        return (PS[b] if dt == F32 else PSB[b]), T(name)

    def end_pass():
        bank[0] = 0
        P.barrier()

    def load_w(dst, Tdst, src, c0, c1, dc0):
        srcv = src.rearrange("(kc p) c -> p kc c", p=128)
        for kc in range(dst.shape[1]):
            P.dma(dst[:, kc, dc0:dc0 + (c1 - c0)], srcv[:, kc, c0:c1], [], [Tdst], Tdst, eng="pool")

    mA = A.mark()
    wA, TwA = tile([128, 8, 5120], BF16, "wA")
    load_w(wA, TwA, w_in, 0, 3072, 0)
    load_w(wA, TwA, w_in, 7168, 9216, 3072)
    m0 = A.mark()
    with ExitStack() as pst:
        ps0 = [PS[i] for i in range(6)]
        Tps0 = [T("ps0") for _ in range(6)]
        ct, Tct = tile([128, 8], F32, "ct")
        ce, Tce = tile([128, 8], F32, "ce")
        cond, Tcond = tile([128, 8], F32, "cond")
        P.dma(ct[:], cT, [], [Tct], Tct)
        P.op("act", lambda e: e.activation(out=ce[:], in_=ct[:], func=AF.Exp, scale=-1.0), r=[Tct], w=[Tce])
        P.op("dve", lambda e: e.tensor_scalar(out=ce[:], in0=ce[:], scalar1=1.0, scalar2=None, op0=ALU.add), r=[Tce], w=[Tce])
        P.op("dve", lambda e: e.reciprocal(out=ce[:], in_=ce[:]), r=[Tce], w=[Tce])
        P.op("dve", lambda e: e.tensor_tensor(out=cond[:], in0=ct[:], in1=ce[:], op=ALU.mult), r=[Tct, Tce], w=[Tcond])
        rowt, Trow = tile([1, 8 * D], F32, "rowt")
        adar, Tadar = tile([1, 6 * D], F32, "adar")
        badar, Tbadar = tile([1, 6 * D], F32, "badar")
        nrm, Tnrm = tile([1, 2 * D], F32, "nrm")
        lbt, Tlbt = tile([1, 2 * D], F32, "lbt")
        lamt, Tlamt = tile([1, 256], F32, "lamt")
        lsm, Tlsm = tile([1, 8], F32, "lsm")
        P.dma(badar[:], b_ada, [], [Tbadar], Tbadar)
        P.dma(nrm[:, 0:D], norm_mix, [], [Tnrm], Tnrm)
        P.dma(nrm[:, D:2 * D], norm_mlp, [], [Tnrm], Tnrm)
        P.dma(lbt[:, 0:D], lb_logits[0:1, :], [], [Tlbt], Tlbt)
        P.dma(lbt[:, D:2 * D], lb_logits[1:2, :], [], [Tlbt], Tlbt)
        P.dma(lamt[:], lamv, [], [Tlamt], Tlamt)
        wst = [tile([128, 3072], F32, "wst") for _ in range(2)]
        it = 0
        for half in range(2):
            for kc in range(8):
                w_t, Tw = wst[it % 2]
                it += 1
                P.dma(w_t[:], w_ada[kc * 128:(kc + 1) * 128, half * 3072:(half + 1) * 3072], [], [Tw], Tw)
                for g in range(6):
                    P.op("pe", lambda e, g=g, kc=kc, w_t=w_t: e.matmul(out=ps0[g][0:1, :], lhsT=cond[:, kc:kc + 1],
                                                                       rhs=w_t[:, g * 512:(g + 1) * 512],
                                                                       start=(kc == 0), stop=(kc == 7)),
                         r=[Tcond, Tw], w=[Tps0[g]])
            for g in range(6):
                c0 = half * 3072 + g * 512
                P.op("dve", lambda e, g=g, c0=c0: e.tensor_tensor(out=adar[:, c0:c0 + 512], in0=ps0[g][0:1, :],
                                                                   in1=badar[:, c0:c0 + 512], op=ALU.add),
                     r=[Tps0[g], Tbadar], w=[Tadar])
        def ad(i):
            return adar[:, i * D:(i + 1) * D]

        def row(i):
            return rowt[:, i * D:(i + 1) * D]
        P.op("dve", lambda e: e.scalar_tensor_tensor(out=row(0), in0=ad(1), scalar=1.0, in1=nrm[:, 0:D], op0=ALU.add, op1=ALU.mult),
             r=[Tadar, Tnrm], w=[Trow])
        P.op("dve", lambda e: e.tensor_copy(out=row(1), in_=ad(0)), r=[Tadar], w=[Trow])
        P.op("dve", lambda e: e.tensor_copy(out=row(2), in_=ad(2)), r=[Tadar], w=[Trow])
        P.op("dve", lambda e: e.scalar_tensor_tensor(out=row(3), in0=ad(4), scalar=1.0, in1=nrm[:, D:2 * D], op0=ALU.add, op1=ALU.mult),
             r=[Tadar, Tnrm], w=[Trow])
        P.op("dve", lambda e: e.tensor_copy(out=row(4), in_=ad(3)), r=[Tadar], w=[Trow])
        P.op("dve", lambda e: e.tensor_copy(out=row(5), in_=ad(5)), r=[Tadar], w=[Trow])
        P.op("dve", lambda e: e.tensor_tensor(out=lbt[:, 0:D], in0=lbt[:, 0:D], in1=lbt[:, D:2 * D], op=ALU.subtract), r=[Tlbt], w=[Tlbt])
        P.op("act", lambda e: e.activation(out=lbt[:, 0:D], in_=lbt[:, 0:D], func=AF.Exp), r=[Tlbt], w=[Tlbt])
        P.op("dve", lambda e: e.tensor_scalar(out=lbt[:, 0:D], in0=lbt[:, 0:D], scalar1=1.0, scalar2=None, op0=ALU.add), r=[Tlbt], w=[Tlbt])
        P.op("dve", lambda e: e.reciprocal(out=row(6), in_=lbt[:, 0:D]), r=[Tlbt], w=[Trow])
        P.op("dve", lambda e: e.tensor_tensor(out=lamt[:, 0:64], in0=lamt[:, 0:64], in1=lamt[:, 64:128], op=ALU.mult), r=[Tlamt], w=[Tlamt])
        P.op("dve", lambda e: e.tensor_tensor(out=lamt[:, 128:192], in0=lamt[:, 128:192], in1=lamt[:, 192:256], op=ALU.mult), r=[Tlamt], w=[Tlamt])
        P.op("dve", lambda e: e.reduce_sum(out=lsm[:, 0:1], in_=lamt[:, 0:64], axis=AX.X), r=[Tlamt], w=[Tlsm])
        P.op("dve", lambda e: e.reduce_sum(out=lsm[:, 1:2], in_=lamt[:, 128:192], axis=AX.X), r=[Tlamt, Tlsm], w=[Tlsm])
        P.op("act", lambda e: e.activation(out=lsm[:, 2:4], in_=lsm[:, 0:2], func=AF.Exp), r=[Tlsm], w=[Tlsm])
        P.op("dve", lambda e: e.tensor_tensor(out=lsm[:, 4:5], in0=lsm[:, 3:4], in1=lsm[:, 2:3], op=ALU.subtract), r=[Tlsm], w=[Tlsm])
        P.op("dve", lambda e: e.memset(row(7), 0.0), r=[], w=[Trow])
        P.op("dve", lambda e: e.tensor_scalar(out=rowt[:, 7 * D:7 * D + 1], in0=lsm[:, 4:5], scalar1=-0.2, scalar2=None, op0=ALU.add),
             r=[Tlsm, Trow], w=[Trow])
        P.dma(ROWS.rearrange("(o r) d -> o (r d)", o=1), rowt[:], [Trow], [T_ROWS], Trow)
        P.dma(neglam[:], ROWS[7:8, 0:1].partition_broadcast(128)[:, 0, :], [T_ROWS], [Tnl], Tnl)
    A.release(m0)
    end_pass()

    def bload(ap_t, T_t, src_row):
        P.dma(ap_t[:], src_row.partition_broadcast(128)[:, 0, :], [T_ROWS], [T_t], T_t)

    def rmsnorm_affine(xt, Tx, ssq, Tssq, junk, Tjunk, tmp, Ttmp, Av, TA, Bv, TB, hout, Thout):
        P.op("dve", lambda e: e.memset(ssq[:, 0:1], 0.0), r=[], w=[Tssq])
        P.op("act", lambda e: e.activation(out=junk[:], in_=xt[:], func=AF.Square, scale=1.0 / 32.0, accum_out=ssq[:, 0:1]),
             r=[Tx, Tssq], w=[Tjunk, Tssq])
        P.op("act", lambda e: e.activation(out=ssq[:, 1:2], in_=ssq[:, 0:1], func=AF.Ln, bias=eps_ap, scale=1.0), r=[Tssq, Tcf], w=[Tssq])
        P.op("act", lambda e: e.activation(out=ssq[:, 2:3], in_=ssq[:, 1:2], func=AF.Exp, scale=-0.5), r=[Tssq], w=[Tssq])
        P.op("dve", lambda e: e.scalar_tensor_tensor(out=tmp[:], in0=xt[:], scalar=ssq[:, 2:3], in1=Av[:], op0=ALU.mult, op1=ALU.mult),
             r=[Tx, Tssq, TA], w=[Ttmp])
        if Bv is not None:
            P.op("dve", lambda e: e.tensor_tensor(out=hout[:], in0=tmp[:], in1=Bv[:], op=ALU.add), r=[Ttmp, TB], w=[Thout])

    with ExitStack() as pst:
        Am, TAm = tile([128, D], F32, "Am")
        Bm, TBm = tile([128, D], F32, "Bm")
        bload(Am, TAm, ROWS[0:1, :])
        bload(Bm, TBm, ROWS[1:2, :])
        pit, Tpit = tile([128, NBL], I32, "pit")
        pft, Tpft = tile([128, NBL], F32, "pft")
        ang, Tang = tile([128, NBL, 8], F32, "ang")
        angc, Tangc = tile([128, NBL, 8], F32, "angc")
        kf, Tkf = tile([128, NBL, 8], F32, "kf")
        ki, Tki = tile([128, NBL, 8], I32, "ki")
        mk, Tmk = tile([128, NBL, 8], F32, "mk")
        cost, Tcos = tile([128, NBL, 8], F32, "cost")
        sint, Tsin = tile([128, NBL, 8], F32, "sint")
        P.dma(pit[:], posl, [], [Tpit], Tpit)
        P.op("dve", lambda e: e.tensor_copy(out=pft[:], in_=pit[:]), r=[Tpit], w=[Tpft])
        P.op("dve", lambda e: e.tensor_tensor(out=ang[:], in0=pft[:].unsqueeze(2).to_broadcast([128, NBL, 8]),
                                              in1=cfs("invf").unsqueeze(1).to_broadcast([128, NBL, 8]), op=ALU.mult),
             r=[Tpft, Tcf], w=[Tang])
        P.op("dve", lambda e: e.tensor_scalar(out=angc[:], in0=ang[:], scalar1=float(np.pi / 2), scalar2=None, op0=ALU.add), r=[Tang], w=[Tangc])
        C1 = 6.28125
        C2 = float(2 * np.pi - 6.28125)

        def sin_of(a_t, Ta, o_t, To):
            P.op("dve", lambda e: e.tensor_scalar(out=kf[:], in0=a_t[:], scalar1=float(1 / (2 * np.pi)), scalar2=None, op0=ALU.mult), r=[Ta], w=[Tkf])
            P.op("dve", lambda e: e.tensor_copy(out=ki[:], in_=kf[:]), r=[Tkf], w=[Tki])
            P.op("dve", lambda e: e.tensor_copy(out=kf[:], in_=ki[:]), r=[Tki], w=[Tkf])
            P.op("dve", lambda e: e.scalar_tensor_tensor(out=a_t[:], in0=kf[:], scalar=-C1, in1=a_t[:], op0=ALU.mult, op1=ALU.add), r=[Tkf, Ta], w=[Ta])
            P.op("dve", lambda e: e.scalar_tensor_tensor(out=a_t[:], in0=kf[:], scalar=-C2, in1=a_t[:], op0=ALU.mult, op1=ALU.add), r=[Tkf, Ta], w=[Ta])
            P.op("dve", lambda e: e.tensor_single_scalar(out=mk[:], in_=a_t[:], scalar=float(np.pi), op=ALU.is_gt), r=[Ta], w=[Tmk])
            P.op("dve", lambda e: e.scalar_tensor_tensor(out=a_t[:], in0=mk[:], scalar=float(-2 * np.pi), in1=a_t[:], op0=ALU.mult, op1=ALU.add), r=[Tmk, Ta], w=[Ta])
            P.op("dve", lambda e: e.tensor_scalar(out=a_t[:], in0=a_t[:], scalar1=float(-np.pi), scalar2=float(np.pi), op0=ALU.max, op1=ALU.min), r=[Ta], w=[Ta])
            P.op("act", lambda e: e.activation(out=o_t[:], in_=a_t[:], func=AF.Sin), r=[Ta], w=[To])
        sin_of(ang, Tang, sint, Tsin)
        sin_of(angc, Tangc, cost, Tcos)

        psT, TpsT = psum([128, 1024], BF16, "psT")
        psKT, TpsKT = psum([128, 1024], BF16, "psKT")
        psP = [psum([128, 512], F32, "psP") for _ in range(4)]
        psG = [psum([128, 512], F32, "psG") for _ in range(2)]
        xs = [tile([128, D], F32, "xs") for _ in range(2)]
        ssqs = [tile([128, 4], F32, "ssq") for _ in range(2)]
        junks = [tile([128, D], BF16, "junk") for _ in range(2)]
        tmps = [tile([128, D], F32, "tmp") for _ in range(2)]
        hbs = [tile([128, D], BF16, "hb") for _ in range(2)]
        hT = [tile([128, D], BF16, "hT") for _ in range(2)]
        ksbs = [tile([128, D], BF16, "ksb") for _ in range(2)]
        qsb, Tqsb = tile([128, D], BF16, "qsb")
        kTs = [tile([128, D], BF16, "kTs") for _ in range(2)]
        qTs = [tile([128, D], BF16, "qTs") for _ in range(2)]
        vsb = [tile([128, D], BF16, "vsb") for _ in range(2)]
        egt, Tegt = tile([128, D], F32, "egt")
        sgs = [tile([128, D], BF16, "sgs") for _ in range(4)]
        rr = [tile([128, 16, 8], F32, "rr") for _ in range(4)]
        pidx = [0]

        def proj_tok(hTt, ThT, wt, Tw, c0, evac):
            for g in range(2):
                ps, Tps = psP[pidx[0] % len(psP)]
                pidx[0] += 1
                for kc in range(8):
                    P.op("pe", lambda e, kc=kc, g=g, ps=ps: e.matmul(out=ps[:], lhsT=hTt[:, kc * 128:(kc + 1) * 128],
                                                                       rhs=wt[:, kc, c0 + g * 512:c0 + (g + 1) * 512],
                                                                       start=(kc == 0), stop=(kc == 7)),
                         r=[ThT, Tw], w=[Tps])
                evac(g, ps, Tps)

        def rope(t, Tt, m):
            tv = t[:].rearrange("p (g d) -> p g d", g=16)
            cs = cost[:, m, :].unsqueeze(1).to_broadcast([128, 16, 8])
            sn = sint[:, m, :].unsqueeze(1).to_broadcast([128, 16, 8])
            t1 = tv[:, :, 0:8]
            t2 = tv[:, :, 8:16]
            (ra, Tra), (rb, Trb), (rc, Trc), (rd, Trd) = rr
            P.op("dve", lambda e: e.tensor_tensor(out=ra[:], in0=t1, in1=cs, op=ALU.mult), r=[Tt, Tcos], w=[Tra])
            P.op("dve", lambda e: e.tensor_tensor(out=rb[:], in0=t2, in1=sn, op=ALU.mult), r=[Tt, Tsin], w=[Trb])
            P.op("dve", lambda e: e.tensor_tensor(out=rc[:], in0=t2, in1=cs, op=ALU.mult), r=[Tt, Tcos], w=[Trc])
            P.op("dve", lambda e: e.tensor_tensor(out=rd[:], in0=t1, in1=sn, op=ALU.mult), r=[Tt, Tsin], w=[Trd])
            P.op("dve", lambda e: e.tensor_tensor(out=t1, in0=ra[:], in1=rb[:], op=ALU.subtract), r=[Tra, Trb, Tt], w=[Tt])
            P.op("dve", lambda e: e.tensor_tensor(out=t2, in0=rc[:], in1=rd[:], op=ALU.add), r=[Trc, Trd, Tt], w=[Tt])

        def transpose8(src, Tsrc, ps, Tps, dst, Tdst, eng):
            for h in range(8):
                P.op("pe", lambda e, h=h: e.transpose(out=ps[:, h * 128:(h + 1) * 128], in_=src[:, h * 128:(h + 1) * 128], identity=cbs("ident")),
                     r=[Tsrc, Tcb], w=[Tps])
            if eng == "act":
                P.op("act", lambda e: e.activation(out=dst[:], in_=ps[:], func=AF.Copy), r=[Tps], w=[Tdst])
            else:
                P.op(eng, lambda e: e.tensor_copy(out=dst[:], in_=ps[:]), r=[Tps], w=[Tdst])

        def copy_evac(dst, Tdst, eng="act"):
            def f(g, ps, Tps):
                if eng == "act":
                    P.op("act", lambda e: e.activation(out=dst[:, g * 512:(g + 1) * 512], in_=ps[:], func=AF.Copy), r=[Tps], w=[Tdst])
                else:
                    P.op(eng, lambda e: e.tensor_copy(out=dst[:, g * 512:(g + 1) * 512], in_=ps[:]), r=[Tps], w=[Tdst])
            return f

        KTv = KT.rearrange("h p n -> p h n")
        QTv = QT.rearrange("h p n -> p h n")
        def pa_block(m, phase):
            s2 = m % 2
            own = (m % 2 == 1)
            i = (m - 1) // 2
            x_t, Tx = xs[s2]
            ssq, Tssq = ssqs[s2]
            junk, Tjunk = junks[s2]
            tmp, Ttmp = tmps[s2]
            hb, Thb = hbs[s2]
            ksb, Tksb = ksbs[s2]
            hTt, ThT = hT[s2]
            if phase == 0:
                P.dma(x_t[:], xl[m * 128:(m + 1) * 128, :], [], [Tx], Tx)
                yield
                rmsnorm_affine(x_t, Tx, ssq, Tssq, junk, Tjunk, tmp, Ttmp, Am, TAm, Bm, TBm, hb, Thb)
                yield
                if m == 0:
                    P.op("pool", lambda e: e.tensor_scalar(out=hb[:], in0=hb[:], scalar1=flag[:, 0:1], scalar2=None, op0=ALU.mult), r=[Thb, Tflag], w=[Thb])
                    yield
                transpose8(hb, Thb, psT, TpsT, hTt, ThT, "act")
                yield
                P.dma(HT[m], hTt[:], [ThT], [T_HT], ThT, eng="pool")
                yield
                return
            proj_tok(hTt, ThT, wA, TwA, 1024, copy_evac(ksb, Tksb, "act"))
            yield
            rope(ksb, Tksb, m)
            yield
            kT_t, TkT = kTs[s2]
            transpose8(ksb, Tksb, psKT, TpsKT, kT_t, TkT, "dve")
            yield
            P.dma(KTv[:, :, m * 128:(m + 1) * 128], kT_t[:].rearrange("p (h n) -> p h n", h=8), [TkT], [T_KT], TkT, eng="pool")
            yield
            v_t, Tv = vsb[s2]
            proj_tok(hTt, ThT, wA, TwA, 2048, copy_evac(v_t, Tv, "dve"))
            yield
            P.dma(Vd[m * 128:(m + 1) * 128, :], v_t[:], [Tv], [T_V], Tv, eng="pool")
            yield
            if own:
                proj_tok(hTt, ThT, wA, TwA, 0, copy_evac(qsb, Tqsb, "act"))
                yield
                rope(qsb, Tqsb, m)
                yield
                qT_t, TqT = qTs[i % 2]
                transpose8(qsb, Tqsb, psKT, TpsKT, qT_t, TqT, "dve")
                yield
                P.dma(QTv[:, :, i * 128:(i + 1) * 128], qT_t[:].rearrange("p (h n) -> p h n", h=8), [TqT], [T_QT], TqT, eng="pool")
                yield
                for gi, (c0, DST, TDST) in enumerate(((3072, SGA, T_SGA), (4096, SGR, T_SGR))):
                    for oc in range(8):
                        ps, Tps = psG[oc // 4]
                        for kc in range(8):
                            P.op("pe", lambda e, oc=oc, kc=kc, ps=ps, c0=c0, hTt=hTt: e.matmul(
                                out=ps[:, (oc % 4) * 128:(oc % 4 + 1) * 128], lhsT=wA[:, kc, c0 + oc * 128:c0 + (oc + 1) * 128],
                                rhs=hTt[:, kc * 128:(kc + 1) * 128], start=(kc == 0), stop=(kc == 7)), r=[TwA, ThT], w=[Tps])
                            yield
                    for g in range(2):
                        ps, Tps = psG[g]
                        P.op("act", lambda e, g=g, ps=ps: e.activation(out=egt[:, g * 512:(g + 1) * 512], in_=ps[:], func=AF.Exp, scale=-1.0),
                             r=[Tps], w=[Tegt])
                        yield
                    P.op("act", lambda e: e.activation(out=egt[:], in_=egt[:], func=AF.Ln, bias=one_ap, scale=1.0), r=[Tegt, Tcf], w=[Tegt])
                    yield
                    sg_t, Tsg = sgs[(2 * i + gi) % 4]
                    P.op("act", lambda e, sg_t=sg_t: e.activation(out=sg_t[:], in_=egt[:], func=AF.Exp, scale=-1.0), r=[Tegt], w=[Tsg])
                    yield
                    P.dma(DST[:, :, i * 128:(i + 1) * 128], sg_t[:].rearrange("p (h n) -> p h n", h=8), [Tsg], [TDST], Tsg, eng="pool")
                    yield
        zipper([pa_block(0, 0)])
        for m in range(NBL):
            zipper(([pa_block(m + 1, 0)] if m + 1 < NBL else []) + [pa_block(m, 1)])
    A.release(mA)
    end_pass()

    with ExitStack() as pst:
        wH, TwH = tile([128, 8, 4096], BF16, "wH")
        load_w(wH, TwH, w_in, 3072, 7168, 0)
        LB, TLB = tile([128, D], F32, "LB")
        bload(LB, TLB, ROWS[6:7, :])
        psP1, TpsP1 = psum([128, 512], F32, "psP1")
        psC, TpsC = psum([128, 16], F32, "psC")
        psX = [psum([128, 512], F32, "psX") for _ in range(2)]
        psQ = [psum([128, 512], F32, "psQ") for _ in range(2)]
        psO = [psum([128, 512], F32, "psO") for _ in range(2)]
        hT = [tile([128, D], BF16, "hT") for _ in range(2)]
        risb = [tile([128, D], BF16, "risb") for _ in range(2)]
        Es = [tile([128, D], F32, "E") for _ in range(2)]
        Ls = [tile([128, D], F32, "L") for _ in range(2)]
        Tts = [tile([128, D], F32, "Tt") for _ in range(2)]
        khats = [tile([128, D], BF16, "khat") for _ in range(2)]
        ebCs = [tile([128, 32], F32, "ebC") for _ in range(2)]
        khcs = [[tile([128, D], BF16, "khc") for _ in range(NCH)] for _ in range(2)]
        Ss = [(A.tile([128, D], F32, "S"), [T("S") for _ in range(8)]) for _ in range(2)]
        cur = [0]
        Sb = [tile([128, D], BF16, "Sb") for _ in range(NCH)]
        epb, Tepb = tile([128, D], F32, "epb")
        enb, Tenb = tile([128, D], F32, "enb")
        kt, Tkt = tile([128, D], BF16, "kt")
        qt, Tqt = tile([128, D], BF16, "qt")
        scm, Tscm = tile([128, D], BF16, "scm")
        sq, Tsq = tile([128, D], BF16, "sq")
        rstdT, TrstdT = tile([128, D], F32, "rstdT")
        gate, Tgate = tile([128, D], F32, "gate")
        on, Ton = tile([128, D], F32, "on")
        ors = [tile([128, D], BF16, "ors") for _ in range(2)]
        P.op("pool", lambda e: e.memset(Ss[0][0][:], 0.0), r=[], w=Ss[0][1])

        def proj_tok1(hTt, ThT, c0, evac):
            for g in range(2):
                for kc in range(8):
                    P.op("pe", lambda e, kc=kc, g=g: e.matmul(out=psP1[:], lhsT=hTt[:, kc * 128:(kc + 1) * 128],
                                                               rhs=wH[:, kc, c0 + g * 512:c0 + (g + 1) * 512],
                                                               start=(kc == 0), stop=(kc == 7)), r=[ThT, TwH], w=[TpsP1])
                evac(g)

        def proj_feat(hTt, ThT, c0, pss):
            for h in range(8):
                ps, Tps = pss[h // 4]
                for kc in range(8):
                    P.op("pe", lambda e, h=h, kc=kc, ps=ps: e.matmul(out=ps[:, (h % 4) * 128:(h % 4 + 1) * 128],
                                                                       lhsT=wH[:, kc, c0 + h * 128:c0 + (h + 1) * 128],
                                                                       rhs=hTt[:, kc * 128:(kc + 1) * 128], start=(kc == 0), stop=(kc == 7)),
                         r=[TwH, ThT], w=[Tps])

        def ph_block(m, phase):
            s2 = m % 2
            own = (m % 2 == 1)
            i = (m - 1) // 2
            E, TE = Es[s2]
            L, TL = Ls[s2]
            Tt, TTt = Tts[s2]
            khat, Tkhat = khats[s2]
            ebC, TebC = ebCs[s2]
            khc = khcs[s2]
            hTt, ThT = hT[s2]
            ri_t, Tri = risb[s2]
            if phase == 0:
                P.dma(hTt[:], HT[m], [T_HT], [ThT], ThT)
                yield
                proj_tok1(hTt, ThT, 1024, lambda g: P.op("act", lambda e: e.activation(out=E[:, g * 512:(g + 1) * 512], in_=psP1[:], func=AF.Exp, scale=-1.0),
                                                         r=[TpsP1], w=[TE]))
                yield
                proj_tok1(hTt, ThT, 2048, lambda g, ri_t=ri_t, Tri=Tri: P.op("dve", lambda e: e.tensor_copy(out=ri_t[:, g * 512:(g + 1) * 512], in_=psP1[:]),
                                                         r=[TpsP1], w=[Tri]))
                yield
                P.op("act", lambda e: e.activation(out=L[:], in_=E[:], func=AF.Ln, bias=one_ap, scale=1.0), r=[TE, Tcf], w=[TL])
                yield
                P.op("dve", lambda e: e.tensor_tensor(out=Tt[:], in0=E[:], in1=LB[:], op=ALU.mult), r=[TE, TLB], w=[TTt])
                yield
                P.op("act", lambda e: e.activation(out=Tt[:], in_=Tt[:], func=AF.Ln, bias=one_ap, scale=1.0), r=[TTt, Tcf], w=[TTt])
                yield
                P.op("dve", lambda e: e.tensor_tensor(out=Tt[:], in0=Tt[:], in1=L[:], op=ALU.subtract), r=[TTt, TL], w=[TTt])
                yield
                P.op("act", lambda e: e.activation(out=E[:], in_=Tt[:], func=AF.Exp), r=[TTt], w=[TE])
                yield
                P.op("dve", lambda e: e.tensor_scalar(out=E[:], in0=E[:], scalar1=-1.0, scalar2=1.0, op0=ALU.mult, op1=ALU.add), r=[TE], w=[TE])
                yield
                for g in range(2):
                    P.op("pe", lambda e, g=g: e.matmul(out=psP1[:], lhsT=cfs("vaft" if own else "vaft128"), rhs=Tt[:, g * 512:(g + 1) * 512], start=True, stop=True),
                         r=[Tcf, TTt], w=[TpsP1])
                    yield
                    P.op("act", lambda e, g=g: e.activation(out=L[:, g * 512:(g + 1) * 512], in_=psP1[:], func=AF.Exp), r=[TpsP1], w=[TL])
                    yield
                P.op("dve", lambda e: e.tensor_tensor(out=khat[:], in0=E[:], in1=L[:], op=ALU.mult), r=[TE, TL], w=[Tkhat])
                yield
                if own:
                    for c in range(NCH):
                        P.op("dve", lambda e, c=c: e.tensor_scalar(out=khc[c][0][:], in0=khat[:], scalar1=cf[:, 384 + c:385 + c], scalar2=None, op0=ALU.mult),
                             r=[Tkhat, Tcf], w=[khc[c][1]])
                        yield
                    for h in range(8):
                        P.op("pe", lambda e, h=h: e.matmul(out=psC[:, NCH * h:NCH * h + NCH], lhsT=Tt[:, h * 128:(h + 1) * 128], rhs=cfs("cind"), start=True, stop=True),
                             r=[TTt, Tcf], w=[TpsC])
                        yield
                    P.op("act", lambda e: e.activation(out=ebC[:], in_=psC[:, 0:32], func=AF.Exp), r=[TpsC], w=[TebC])
                    yield
                else:
                    for h in range(8):
                        P.op("pe", lambda e, h=h: e.matmul(out=psC[:, h:h + 1], lhsT=Tt[:, h * 128:(h + 1) * 128], rhs=one_ap, start=True, stop=True),
                             r=[TTt, Tcf], w=[TpsC])
                        yield
                    P.op("act", lambda e: e.activation(out=ebC[:, 0:8], in_=psC[:, 0:8], func=AF.Exp), r=[TpsC], w=[TebC])
                    yield
                return
            if own:
                S0, TS0 = Ss[cur[0]]
                P.op("act", lambda e, S0=S0: e.activation(out=Sb[0][0][:], in_=S0[:], func=AF.Copy), r=TS0, w=[Sb[0][1]])
                yield
            for c in range(NCH if own else 1):
                kx, Tkx = khc[c] if own else (khat, Tkhat)
                for h in range(8):
                    ps, Tps = psX[h // 4]
                    P.op("pe", lambda e, h=h, ps=ps, ri_t=ri_t, kx=kx: e.matmul(out=ps[:, (h % 4) * 128:(h % 4 + 1) * 128],
                                                                     lhsT=kx[:, h * 128:(h + 1) * 128],
                                                                     rhs=ri_t[:, h * 128:(h + 1) * 128], start=True, stop=True),
                         r=[Tkx, Tri], w=[Tps])
                    yield
                Sa, TSa = Ss[cur[0]]
                Sn, TSn = Ss[1 - cur[0]]
                cur[0] = 1 - cur[0]
                for h in range(8):
                    ps, Tps = psX[h // 4]
                    col = (NCH * h + c) if own else h
                    P.op("dve", lambda e, h=h, ps=ps, Sa=Sa, Sn=Sn, col=col: e.scalar_tensor_tensor(out=Sn[:, h * 128:(h + 1) * 128], in0=Sa[:, h * 128:(h + 1) * 128],
                                                                                    scalar=ebC[:, col:col + 1],
                                                                                    in1=ps[:, (h % 4) * 128:(h % 4 + 1) * 128], op0=ALU.mult, op1=ALU.add),
                         r=[TSa[h], TebC, Tps], w=[TSn[h]])
                    yield
                if own and c < NCH - 1:
                    P.op("act", lambda e, c=c, Sn=Sn: e.activation(out=Sb[c + 1][0][:], in_=Sn[:], func=AF.Copy), r=TSn, w=[Sb[c + 1][1]])
                    yield
            if not own:
                return
            for h in range(8):
                ps, Tps = psX[h // 4]
                P.op("pe", lambda e, h=h, ps=ps: e.matmul(out=ps[:, (h % 4) * 128:(h % 4 + 1) * 128], lhsT=Tt[:, h * 128:(h + 1) * 128],
                                                            rhs=cfs("uincl"), start=True, stop=True), r=[TTt, Tcf], w=[Tps])
                yield
            for g in range(2):
                ps, Tps = psX[g]
                P.op("act", lambda e, g=g, ps=ps: e.activation(out=epb[:, g * 512:(g + 1) * 512], in_=ps[:], func=AF.Exp), r=[Tps], w=[Tepb])
                yield
                P.op("act", lambda e, g=g, ps=ps: e.activation(out=enb[:, g * 512:(g + 1) * 512], in_=ps[:], func=AF.Exp, scale=-1.0), r=[Tps], w=[Tenb])
                yield
            for h in range(8):
                ps, Tps = psX[h // 4]
                P.op("pe", lambda e, h=h, ps=ps: e.transpose(out=ps[:, (h % 4) * 128:(h % 4 + 1) * 128], in_=E[:, h * 128:(h + 1) * 128],
                                                               identity=cfs("identf")), r=[TE, Tcf], w=[Tps])
                yield
            for g in range(2):
                ps, Tps = psX[g]
                P.op("dve", lambda e, g=g, ps=ps: e.tensor_tensor(out=kt[:, g * 512:(g + 1) * 512], in0=ps[:], in1=enb[:, g * 512:(g + 1) * 512], op=ALU.mult),
                     r=[Tps, Tenb], w=[Tkt])
                yield
            proj_feat(hTt, ThT, 0, psQ)
            yield
            for g in range(2):
                ps, Tps = psQ[g]
                P.op("dve", lambda e, g=g, ps=ps: e.tensor_tensor(out=qt[:, g * 512:(g + 1) * 512], in0=ps[:], in1=epb[:, g * 512:(g + 1) * 512], op=ALU.mult),
                     r=[Tps, Tepb], w=[Tqt])
                yield
            for h in range(8):
                ps, Tps = psX[h // 4]
                P.op("pe", lambda e, h=h, ps=ps: e.matmul(out=ps[:, (h % 4) * 128:(h % 4 + 1) * 128], lhsT=kt[:, h * 128:(h + 1) * 128],
                                                            rhs=qt[:, h * 128:(h + 1) * 128], start=True, stop=True), r=[Tkt, Tqt], w=[Tps])
                yield
            for g in range(2):
                ps, Tps = psX[g]
                P.op("dve", lambda e, g=g, ps=ps: e.tensor_tensor(out=scm[:, g * 512:(g + 1) * 512], in0=ps[:], in1=cbs("m01"), op=ALU.mult),
                     r=[Tps, Tcb], w=[Tscm])
                yield
            for h in range(8):
                ps, Tps = psO[h // 4]
                for c in range(NCH):
                    c0 = (h % 4) * 128 + c * CH
                    P.op("pe", lambda e, h=h, c=c, ps=ps, c0=c0, ri_t=ri_t: e.matmul(out=ps[:, c0:c0 + CH], lhsT=ri_t[:, h * 128:(h + 1) * 128],
                                                                            rhs=scm[:, h * 128 + c * CH:h * 128 + (c + 1) * CH], start=True, stop=False),
                         r=[Tri, Tscm], w=[Tps])
                    yield
                    P.op("pe", lambda e, h=h, c=c, ps=ps, c0=c0: e.matmul(out=ps[:, c0:c0 + CH], lhsT=Sb[c][0][:, h * 128:(h + 1) * 128],
                                                                            rhs=qt[:, h * 128 + c * CH:h * 128 + (c + 1) * CH], start=False, stop=True),
                         r=[Sb[c][1], Tqt], w=[Tps])
                    yield
            for g in range(2):
                ps, Tps = psO[g]
                P.op("act", lambda e, g=g, ps=ps: e.activation(out=sq[:, g * 512:(g + 1) * 512], in_=ps[:], func=AF.Square), r=[Tps], w=[Tsq])
                yield
            for g in range(2):
                ps, Tps = psX[g]
                P.op("pe", lambda e, g=g, ps=ps: e.matmul(out=ps[:], lhsT=cbs("onesm"), rhs=sq[:, g * 512:(g + 1) * 512], start=True, stop=True),
                     r=[Tcb, Tsq], w=[Tps])
                yield
                P.op("act", lambda e, g=g, ps=ps: e.activation(out=rstdT[:, g * 512:(g + 1) * 512], in_=ps[:], func=AF.Ln, bias=eps_ap, scale=1.0),
                     r=[Tps, Tcf], w=[TrstdT])
                yield
            P.op("act", lambda e: e.activation(out=rstdT[:], in_=rstdT[:], func=AF.Exp, scale=-0.5), r=[TrstdT], w=[TrstdT])
            yield
            proj_feat(hTt, ThT, 3072, psQ)
            yield
            for g in range(2):
                ps, Tps = psQ[g]
                P.op("act", lambda e, g=g, ps=ps: e.activation(out=gate[:, g * 512:(g + 1) * 512], in_=ps[:], func=AF.Exp, scale=-1.0), r=[Tps], w=[Tgate])
                yield
            P.op("act", lambda e: e.activation(out=gate[:], in_=gate[:], func=AF.Ln, bias=one_ap, scale=1.0), r=[Tgate, Tcf], w=[Tgate])
            yield
            P.op("act", lambda e: e.activation(out=gate[:], in_=gate[:], func=AF.Exp, scale=-1.0), r=[Tgate], w=[Tgate])
            yield
            for g in range(2):
                ps, Tps = psQ[g]
                P.op("dve", lambda e, g=g, ps=ps: e.tensor_tensor(out=gate[:, g * 512:(g + 1) * 512], in0=ps[:], in1=gate[:, g * 512:(g + 1) * 512], op=ALU.mult),
                     r=[Tps, Tgate], w=[Tgate])
                yield
            for g in range(2):
                ps, Tps = psO[g]
                P.op("dve", lambda e, g=g, ps=ps: e.scalar_tensor_tensor(out=on[:, g * 512:(g + 1) * 512], in0=ps[:], scalar=recw[:, 0:1],
                                                                           in1=rstdT[:, g * 512:(g + 1) * 512], op0=ALU.mult, op1=ALU.mult),
                     r=[Tps, Trecw, TrstdT], w=[Ton])
                yield
            or_t, Tor = ors[i % 2]
            P.op("dve", lambda e, or_t=or_t: e.tensor_tensor(out=or_t[:], in0=on[:], in1=gate[:], op=ALU.mult), r=[Ton, Tgate], w=[Tor])
            yield
            P.dma(ORT[:, :, i * 128:(i + 1) * 128], or_t[:].rearrange("p (h n) -> p h n", h=8), [Tor], [T_ORT], Tor, eng="pool")
            yield
        zipper([ph_block(0, 0)])
        for m in range(NBL):
            zipper(([ph_block(m + 1, 0)] if m + 1 < NBL else []) + [ph_block(m, 1)])
    A.release(mA)
    end_pass()

    OAT, TOAT = tile([128, 8, NQ], BF16, "OAT")
    mB = A.mark()
    with ExitStack() as pst:
        psS2 = []
        psS2b = []
        for _ in range(2):
            pr_ = [psum([128, 512], F32, "psS2") for _ in range(2)]
            psS2.append(pr_)
            psS2b.append(last_bank[0] - 1)
        psO = [psum([128, 512], F32, "psOa") for _ in range(2)]
        psD = [psum([128, 512], F32, "psDa") for _ in range(2)]
        kth = [tile([128, NTOK], BF16, "kth") for _ in range(2)]
        vth = [tile([128, NBL, 128], BF16, "vth") for _ in range(2)]
        qth = [tile([128, NQ], BF16, "qth") for _ in range(2)]
        PT = [tile([128, 2, 512], BF16, "PT") for _ in range(3)]
        onesb, Tonesb = tile([128, 128], BF16, "onesb")
        subw8, Tsubw8 = tile([128, 1], F32, "subw8")
        rr1, Trr1 = tile([128, 512], F32, "rr1")
        rr2, Trr2 = tile([128, 512], F32, "rr2")
        tt1, Ttt1 = tile([128, 512], F32, "tt1")
        oa, Toa = tile([128, 512], F32, "oa")
        sqb, Tsqb = tile([128, 512], BF16, "sqb")
        rsd, Trsd = tile([128, 512], F32, "rsd")
        P.op("dve", lambda e: e.memset(onesb[:], 1.0), r=[], w=[Tonesb])
        P.op("dve", lambda e: e.tensor_scalar(out=subw8[:], in0=subw[:], scalar1=0.8, scalar2=None, op0=ALU.mult), r=[Tsubw], w=[Tsubw8])
        Vv = Vd.rearrange("(m p) c -> p m c", p=128)
        NG = (NOWN + 3) // 4
        LOOK = 1
        units = []

        def load_head(h):
            k_t, Tk = kth[h % 2]
            v_t, Tv = vth[h % 2]
            q_t, Tq = qth[h % 2]
            P.dma(k_t[:], KT[h], [T_KT], [Tk], Tk)
            for mb in range(0, NBL, 13):
                me = min(NBL, mb + 13)
                P.dma(v_t[:, mb:me, :], Vv[:, mb:me, h * 128:(h + 1) * 128], [T_V], [Tv], Tv)
            P.dma(q_t[:], QT[h], [T_QT], [Tq], Tq)

        def make_unit(h, i0, nj, kb, nkb, uidx, pre):
            k_t, Tk = kth[h % 2]
            v_t, Tv = vth[h % 2]
            q_t, Tq = qth[h % 2]
            pss = psS2[uidx % 2]
            pt, Tpt = PT[uidx % 3]
            jmin = max(0, kb // 2 - i0)
            diag = (kb % 2 == 1) and ((kb - 1) // 2 - i0 >= 0)
            qbase = i0 * 128
            c0, c1 = jmin * 128, nj * 128

            def qk():
                for mp in range(2):
                    ps, Tps = pss[mp]
                    kl = k_t[mp * 64:(mp + 1) * 64, kb * 128:(kb + 1) * 128]

                    def qsl(a, b, mp=mp):
                        return q_t[mp * 64:(mp + 1) * 64, qbase + a * 128:qbase + b * 128]
                    if diag:
                        jd = jmin
                        P.op("pe", lambda e, ps=ps, kl=kl, qs=qsl(jd, jd + 1): e.matmul(out=ps[:, jd * 128:(jd + 1) * 128], lhsT=kl, rhs=qs, start=True, stop=False),
                             r=[Tk, Tq], w=[Tps])
                        P.op("pe", lambda e, ps=ps: e.matmul(out=ps[:, jd * 128:(jd + 1) * 128], lhsT=cbs("ident"), rhs=cbs("caus"), start=False, stop=True),
                             r=[Tcb], w=[Tps])
                        if jd + 1 < nj:
                            P.op("pe", lambda e, ps=ps, kl=kl, qs=qsl(jd + 1, nj): e.matmul(out=ps[:, (jd + 1) * 128:nj * 128], lhsT=kl, rhs=qs, start=True, stop=True),
                                 r=[Tk, Tq], w=[Tps])
                    else:
                        P.op("pe", lambda e, ps=ps, kl=kl, qs=qsl(jmin, nj): e.matmul(out=ps[:, c0:c1], lhsT=kl, rhs=qs, start=True, stop=True),
                             r=[Tk, Tq], w=[Tps])

            def ex():
                b0 = psS2b[uidx % 2]
                src = PS_all[:, b0 * 512:(b0 + 2) * 512].rearrange("p (m n) -> p m n", m=2)[:, :, c0:c1]
                Tp0, Tp1 = pss[0][1], pss[1][1]
                if kb == 0:
                    P.op("act", lambda e: e.activation(out=pt[:, :, c0:c1], in_=src, func=AF.Exp, bias=kb0[:, 0:1], scale=0.125),
                         r=[Tp0, Tp1, Tkb0], w=[Tpt])
                else:
                    P.op("act", lambda e: e.activation(out=pt[:, :, c0:c1], in_=src, func=AF.Exp, scale=0.125),
                         r=[Tp0, Tp1], w=[Tpt])

            def pv():
                if pre is not None:
                    pre()
                for mp in range(2):
                    po, Tpo = psO[mp]
                    pd, Tpd = psD[mp]
                    P.op("pe", lambda e, po=po, mp=mp: e.matmul(out=po[:, c0:c1], lhsT=v_t[:, kb, :], rhs=pt[:, mp, c0:c1], start=(kb == 0), stop=(kb == nkb - 1)),
                         r=[Tv, Tpt], w=[Tpo])
                    P.op("pe", lambda e, pd=pd, mp=mp: e.matmul(out=pd[:, c0:c1], lhsT=onesb[:], rhs=pt[:, mp, c0:c1], start=(kb == 0), stop=(kb == nkb - 1)),
                         r=[Tonesb, Tpt], w=[Tpd])
            return qk, ex, pv

        def make_post(h, i0, nj):
            w = nj * 128

            def post():
                (po1, Tpo1), (po2, Tpo2) = psO
                (pd1, Tpd1), (pd2, Tpd2) = psD
                P.op("act", lambda e: e.activation(out=rr1[:, 0:w], in_=pd1[:, 0:w], func=AF.Ln), r=[Tpd1], w=[Trr1])
                P.op("act", lambda e: e.activation(out=rr1[:, 0:w], in_=rr1[:, 0:w], func=AF.Exp, scale=-1.0), r=[Trr1], w=[Trr1])
                P.op("act", lambda e: e.activation(out=rr2[:, 0:w], in_=pd2[:, 0:w], func=AF.Ln), r=[Tpd2], w=[Trr2])
                P.op("act", lambda e: e.activation(out=rr2[:, 0:w], in_=rr2[:, 0:w], func=AF.Exp, scale=-1.0), r=[Trr2], w=[Trr2])
                P.op("dve", lambda e: e.tensor_tensor(out=tt1[:, 0:w], in0=po1[:, 0:w], in1=rr1[:, 0:w], op=ALU.mult), r=[Tpo1, Trr1], w=[Ttt1])
                P.op("dve", lambda e: e.tensor_tensor(out=rr2[:, 0:w], in0=po2[:, 0:w], in1=rr2[:, 0:w], op=ALU.mult), r=[Tpo2, Trr2], w=[Trr2])
                P.op("dve", lambda e: e.scalar_tensor_tensor(out=oa[:, 0:w], in0=rr2[:, 0:w], scalar=neglam[:, 0:1], in1=tt1[:, 0:w], op0=ALU.mult, op1=ALU.add),
                     r=[Trr2, Tnl, Ttt1], w=[Toa])
                P.op("act", lambda e: e.activation(out=sqb[:, 0:w], in_=oa[:, 0:w], func=AF.Square), r=[Toa], w=[Tsqb])
                P.op("pe", lambda e: e.matmul(out=pd1[:, 0:w], lhsT=cbs("onesm"), rhs=sqb[:, 0:w], start=True, stop=True), r=[Tcb, Tsqb], w=[Tpd1])
                P.op("act", lambda e: e.activation(out=rsd[:, 0:w], in_=pd1[:, 0:w], func=AF.Ln, bias=eps_ap, scale=1.0), r=[Tpd1, Tcf], w=[Trsd])
                P.op("act", lambda e: e.activation(out=rsd[:, 0:w], in_=rsd[:, 0:w], func=AF.Exp, scale=-0.5), r=[Trsd], w=[Trsd])
                P.op("dve", lambda e: e.scalar_tensor_tensor(out=OAT[:, h, i0 * 128:i0 * 128 + w], in0=oa[:, 0:w], scalar=subw8[:, 0:1], in1=rsd[:, 0:w],
                                                             op0=ALU.mult, op1=ALU.mult), r=[Toa, Tsubw8, Trsd], w=[TOAT])
            return post

        uidx = 0
        for h in range(8):
            first = True
            for G in range(NG):
                i0 = 4 * G
                nj = min(4, NOWN - i0)
                nkb = 2 * (i0 + nj - 1) + 2
                for kb in range(nkb):
                    pre = None
                    if first:
                        first = False
                        if h >= 1 and h + 1 < 8:
                            pre = (lambda h=h: load_head(h + 1))
                    qk, ex, pv = make_unit(h, i0, nj, kb, nkb, uidx, pre)
                    post = make_post(h, i0, nj) if kb == nkb - 1 else None
                    units.append((qk, ex, pv, post))
                    uidx += 1
        NU = len(units)
        load_head(0)
        load_head(1)
        for t in range(NU + LOOK):
            if t < NU:
                units[t][0]()
                units[t][1]()
            if t - LOOK >= 0:
                u = units[t - LOOK]
                u[2]()
                if u[3] is not None:
                    u[3]()
    if dbg:
        OATd = nc.dram_tensor("OATd", [128, 8, NQ], BF16, kind="ExternalOutput").ap()
        P.dma(OATd, OAT[:], [TOAT], [T_out], TOAT)
    A.release(mB)
    end_pass()

    with ExitStack() as pst:
        wP, TwP = tile([128, 8, 3 * D], BF16, "wP")
        load_w(wP, TwP, w_pa, 0, D, 0)
        load_w(wP, TwP, w_pr, 0, D, D)
        load_w(wP, TwP, w_o, 0, D, 2 * D)
        Gm, TGm = tile([128, D], F32, "Gm")
        bload(Gm, TGm, ROWS[2:3, :])
        psA = [psum([128, 256], F32, "psA") for _ in range(2)]
        psR = [psum([128, 256], F32, "psR") for _ in range(2)]
        psZ = [psum([128, 512], F32, "psZ") for _ in range(2)]
        NT = 256
        ort = [tile([128, 8, NT], BF16, "ort") for _ in range(2)]
        sga = [tile([128, 8, NT], BF16, "sga") for _ in range(2)]
        sgr = [tile([128, 8, NT], BF16, "sgr") for _ in range(2)]
        y1, Ty1 = tile([128, NT], F32, "y1")
        y2, Ty2 = tile([128, NT], F32, "y2")
        yT, TyT = tile([128, 8, NT], BF16, "yT")
        xs = [tile([128, D], F32, "xs") for _ in range(2)]
        zt, Tzt = tile([128, D], F32, "zt")
        x1s = [tile([128, D], F32, "x1s") for _ in range(2)]
        ngrp = (NQ + NT - 1) // NT
        bi = 0
        for gq in range(ngrp):
            q0 = gq * NT
            nt = min(NT, NQ - q0)
            o_t, To = ort[gq % 2]
            a_t, Ta = sga[gq % 2]
            r_t, Tr = sgr[gq % 2]
            P.dma(o_t[:, :, 0:nt], ORT[:, :, q0:q0 + nt], [T_ORT], [To], To)
            P.dma(a_t[:, :, 0:nt], SGA[:, :, q0:q0 + nt], [T_SGA], [Ta], Ta)
            P.dma(r_t[:, :, 0:nt], SGR[:, :, q0:q0 + nt], [T_SGR], [Tr], Tr)
            for oc in range(8):
                pa, Tpa = psA[oc % 2]
                pr, Tpr = psR[oc % 2]
                for hh in range(8):
                    P.op("pe", lambda e, oc=oc, hh=hh, pa=pa, q0=q0, nt=nt: e.matmul(out=pa[:, 0:nt], lhsT=wP[:, hh, oc * 128:(oc + 1) * 128],
                                                                                       rhs=OAT[:, hh, q0:q0 + nt], start=(hh == 0), stop=(hh == 7)),
                         r=[TwP, TOAT], w=[Tpa])
                for hh in range(8):
                    P.op("pe", lambda e, oc=oc, hh=hh, pr=pr, o_t=o_t, nt=nt: e.matmul(out=pr[:, 0:nt], lhsT=wP[:, hh, D + oc * 128:D + (oc + 1) * 128],
                                                                                         rhs=o_t[:, hh, 0:nt], start=(hh == 0), stop=(hh == 7)),
                         r=[TwP, To], w=[Tpr])
                P.op("dve", lambda e, oc=oc, pa=pa, a_t=a_t, nt=nt: e.tensor_tensor(out=y1[:, 0:nt], in0=pa[:, 0:nt], in1=a_t[:, oc, 0:nt], op=ALU.mult),
                     r=[Tpa, Ta], w=[Ty1])
                P.op("dve", lambda e, oc=oc, pr=pr, r_t=r_t, nt=nt: e.tensor_tensor(out=y2[:, 0:nt], in0=pr[:, 0:nt], in1=r_t[:, oc, 0:nt], op=ALU.mult),
                     r=[Tpr, Tr], w=[Ty2])
                P.op("dve", lambda e, oc=oc, nt=nt: e.tensor_tensor(out=yT[:, oc, 0:nt], in0=y1[:, 0:nt], in1=y2[:, 0:nt], op=ALU.add),
                     r=[Ty1, Ty2], w=[TyT])
            for jb in range(nt // 128):
                i = (q0 // 128) + jb
                m = 2 * i + 1
                x_t, Tx = xs[bi % 2]
                x1_t, Tx1 = x1s[bi % 2]
                bi += 1
                P.dma(x_t[:], xl[m * 128:(m + 1) * 128, :], [], [Tx], Tx)
                for g in range(2):
                    pz, Tpz = psZ[g]
                    for oc in range(8):
                        P.op("pe", lambda e, g=g, oc=oc, pz=pz, jb=jb: e.matmul(out=pz[:], lhsT=yT[:, oc, jb * 128:(jb + 1) * 128],
                                                                                  rhs=wP[:, oc, 2 * D + g * 512:2 * D + (g + 1) * 512],
                                                                                  start=(oc == 0), stop=(oc == 7)), r=[TyT, TwP], w=[Tpz])
                    P.op("dve", lambda e, g=g, pz=pz: e.tensor_tensor(out=zt[:, g * 512:(g + 1) * 512], in0=pz[:], in1=Gm[:, g * 512:(g + 1) * 512], op=ALU.mult),
                         r=[Tpz, TGm], w=[Tzt])
                P.op("dve", lambda e, x_t=x_t, x1_t=x1_t: e.tensor_tensor(out=x1_t[:], in0=zt[:], in1=x_t[:], op=ALU.add), r=[Tzt, Tx], w=[Tx1])
                P.dma(X1[i * 128:(i + 1) * 128, :], x1_t[:], [Tx1], [T_X1], Tx1, eng="pool")
    A.release(mA)
    end_pass()

    with ExitStack() as pst:
        wI, TwI = tile([128, 8, 4 * D], BF16, "wI")
        wO, TwO = tile([128, 32, D], BF16, "wO")
        load_w(wI, TwI, w_mi, 0, 2048, 0)
        load_w(wI, TwI, w_mi, 2048, 4096, 2048)
        wov = w_mo.rearrange("(fc p) c -> p fc c", p=128)
        for fc in range(32):
            P.dma(wO[:, fc, :], wov[:, fc, :], [], [TwO], TwO, eng="pool")
        Af, TAf = tile([128, D], F32, "Af")
        Bf, TBf = tile([128, D], F32, "Bf")
        Gf, TGf = tile([128, D], F32, "Gf")
        NF, TNF = tile([128, D], F32, "NF")
        bload(Af, TAf, ROWS[3:4, :])
        bload(Bf, TBf, ROWS[4:5, :])
        bload(Gf, TGf, ROWS[5:6, :])
        P.dma(NF[:], norm_final.partition_broadcast(128)[:, 0, :], [], [TNF], TNF)
        psT2, TpsT2 = psum([128, 1024], BF16, "psT2")
        psU = [psum([128, 512], F32, "psU") for _ in range(3)]
        psM = [psum([128, 512], F32, "psM") for _ in range(4)]
        NT = 256
        x1t = [tile([128, D], F32, "x1t") for _ in range(3)]
        ssq, Tssq = tile([128, 8], F32, "ssq2")
        junk, Tjunk = tile([128, D], BF16, "junk2")
        tmp, Ttmp = tile([128, D], F32, "tmp2")
        h2, Th2 = tile([128, D], BF16, "h2")
        h2T, Th2T = tile([128, 8, NT], BF16, "h2T")
        rl = [tile([128, 512], BF16, "rl") for _ in range(2)]
        uT, TuT = tile([128, 32, NT], BF16, "uT")
        x2, Tx2 = tile([128, D], F32, "x2")
        ot = [tile([128, D], F32, "ot") for _ in range(2)]
        ngrp = (NQ + NT - 1) // NT
        xi = 0
        ui = 0
        mi = 0
        oi = 0
        for gq in range(ngrp):
            q0 = gq * NT
            nt = min(NT, NQ - q0)
            nb = nt // 128
            xt_l = []
            for jb in range(nb):
                x_t, Tx = x1t[xi % 3]
                xi += 1
                xt_l.append((x_t, Tx))
                i = q0 // 128 + jb
                P.dma(x_t[:], X1[i * 128:(i + 1) * 128, :], [T_X1], [Tx], Tx)
                rmsnorm_affine(x_t, Tx, ssq, Tssq, junk, Tjunk, tmp, Ttmp, Af, TAf, Bf, TBf, h2, Th2)
                for kc in range(8):
                    P.op("pe", lambda e, kc=kc: e.transpose(out=psT2[:, kc * 128:(kc + 1) * 128], in_=h2[:, kc * 128:(kc + 1) * 128], identity=cbs("ident")),
                         r=[Th2, Tcb], w=[TpsT2])
                P.op("act", lambda e, jb=jb: e.activation(out=h2T[:, :, jb * 128:(jb + 1) * 128], in_=psT2[:].rearrange("p (k n) -> p k n", k=8), func=AF.Copy),
                     r=[TpsT2], w=[Th2T])
            for f2 in range(16):
                pu, Tpu = psU[ui % 3]
                r_t, Trl = rl[ui % 2]
                ui += 1
                for ff in range(2):
                    fc = 2 * f2 + ff
                    for kc in range(8):
                        P.op("pe", lambda e, fc=fc, ff=ff, kc=kc, pu=pu, nt=nt: e.matmul(out=pu[:, ff * 256:ff * 256 + nt], lhsT=wI[:, kc, fc * 128:(fc + 1) * 128],
                                                                                           rhs=h2T[:, kc, 0:nt], start=(kc == 0), stop=(kc == 7)),
                             r=[TwI, Th2T], w=[Tpu])
                P.op("act", lambda e, pu=pu, r_t=r_t: e.activation(out=r_t[:], in_=pu[:], func=AF.Relu), r=[Tpu], w=[Trl])
                P.op("dve", lambda e, f2=f2, r_t=r_t: e.tensor_tensor(out=uT[:, 2 * f2:2 * f2 + 2, :], in0=r_t[:].rearrange("p (f n) -> p f n", f=2),
                                                                         in1=r_t[:].rearrange("p (f n) -> p f n", f=2), op=ALU.mult),
                     r=[Trl], w=[TuT])
            for jb in range(nb):
                x_t, Tx = xt_l[jb]
                i = q0 // 128 + jb
                pms = []
                for g in range(2):
                    pm, Tpm = psM[mi % 4]
                    mi += 1
                    pms.append((pm, Tpm))
                    for fc in range(32):
                        P.op("pe", lambda e, g=g, fc=fc, pm=pm, jb=jb: e.matmul(out=pm[:], lhsT=uT[:, fc, jb * 128:(jb + 1) * 128],
                                                                                  rhs=wO[:, fc, g * 512:(g + 1) * 512], start=(fc == 0), stop=(fc == 31)),
                             r=[TuT, TwO], w=[Tpm])
                for g in range(2):
                    pm, Tpm = pms[g]
                    P.op("dve", lambda e, g=g, pm=pm: e.tensor_tensor(out=tmp[:, g * 512:(g + 1) * 512], in0=pm[:], in1=Gf[:, g * 512:(g + 1) * 512], op=ALU.mult),
                         r=[Tpm, TGf], w=[Ttmp])
                P.op("dve", lambda e, x_t=x_t: e.tensor_tensor(out=x2[:], in0=tmp[:], in1=x_t[:], op=ALU.add), r=[Ttmp, Tx], w=[Tx2])
                o_t, Tot = ot[oi % 2]
                oi += 1
                rmsnorm_affine(x2, Tx2, ssq, Tssq, junk, Tjunk, o_t, Tot, NF, TNF, None, None, None, None)
                P.dma(out[i * 128:(i + 1) * 128, :], o_t[:], [Tot], [T_out], Tot, eng="pool")
    fin = Tl("fin")
    P.op("sp", lambda e: e.nop(), r=[T_out], w=[])
    st = ExitStack()
    st.enter_context(nc.allow_low_precision("bf16 operands, fp32 accumulation"))
    st.enter_context(nc.allow_non_contiguous_dma("strided scratch layouts"))
    P.emit(st)
    st.close()
    return nc


def shard_inputs(inp, NBL):
    cf, cb = make_consts()
    x = np.asarray(inp["x"], np.float32)
    B = x.shape[0]
    pos = np.asarray(inp["positions"], np.int32)
    S = x.shape[1]
    NTOK = NBL * 128
    lamv = np.concatenate([np.asarray(inp[k], np.float32).reshape(1, 64) for k in ("lam_q1", "lam_k1", "lam_q2", "lam_k2")], axis=1)
    common = {
        "w_ada": np.ascontiguousarray(inp["w_ada"][0], np.float32), "b_ada": np.ascontiguousarray(inp["b_ada"], np.float32).reshape(1, -1),
        "norm_mix": np.asarray(inp["norm_mix"], np.float32).reshape(1, -1), "w_in": np.ascontiguousarray(inp["w_in"][0], np.float32),
        "lamv": np.ascontiguousarray(lamv), "subln_w": np.asarray(inp["subln_w"], np.float32).reshape(128, 1),
        "lb_logits": np.ascontiguousarray(inp["lb_logits"], np.float32), "rec_norm_w": np.asarray(inp["rec_norm_w"], np.float32).reshape(128, 1),
        "w_proj_att": np.ascontiguousarray(inp["w_proj_att"][0], np.float32), "w_proj_rec": np.ascontiguousarray(inp["w_proj_rec"][0], np.float32),
        "w_out": np.ascontiguousarray(inp["w_out"][0], np.float32), "norm_mlp": np.asarray(inp["norm_mlp"], np.float32).reshape(1, -1),
        "w_mlp_in": np.ascontiguousarray(inp["w_mlp_in"][0], np.float32), "w_mlp_out": np.ascontiguousarray(inp["w_mlp_out"][0], np.float32),
        "norm_final": np.asarray(inp["norm_final"], np.float32).reshape(1, -1), "cf": cf, "cb": cb,
    }
    maps = []
    for b in range(B):
        for p in range(2):
            start = 128 * (p - 1)
            xl = np.zeros((NTOK, D), np.float32)
            pl = np.zeros((NTOK,), np.int32)
            lo = max(0, start)
            hi = min(S, start + NTOK)
            xl[lo - start:hi - start] = x[b, lo:hi]
            pl[lo - start:hi - start] = pos[b, lo:hi]
            d = dict(common)
            d["xl"] = xl
            d["posl"] = np.ascontiguousarray(pl.reshape(NBL, 128).T)
            d["cT"] = np.ascontiguousarray(np.asarray(inp["c"], np.float32)[b].reshape(8, 128).T)
            d["flag"] = np.full((128, 1), float(p), np.float32)
            d["kb0"] = np.full((128, 1), 0.0 if p == 1 else NEG, np.float32)
            maps.append(d)
    return maps


def gather(results, B, NBL):
    NOWN = (NBL - 1) // 2
    S = 2 * NOWN * 128
    out = np.zeros((B, S, D), np.float32)
    for b in range(B):
        for p in range(2):
            o = np.asarray(results[2 * b + p]["out"]).reshape(NOWN, 128, D)
            ov = out[b].reshape(NOWN, 2, 128, D)
            ov[:, p] = o
    return out


_NC_CACHE = {}


def kernel(**inputs):
    NBL = 65
    if NBL not in _NC_CACHE:
        _NC_CACHE[NBL] = build(NBL)
    nc = _NC_CACHE[NBL]
    maps = shard_inputs(inputs, NBL)
    res = run_bass_kernel_spmd(nc, maps, core_ids=list(range(8)))
    return gather(res.results, 4, NBL)
```

```python
import numpy as np
import ml_dtypes
from contextlib import ExitStack
import concourse.bass as bass
import concourse.mybir as mybir
from concourse.bass_utils import run_bass_kernel_spmd

F32 = mybir.dt.float32
BF16 = mybir.dt.bfloat16
I32 = mybir.dt.int32
AF = mybir.ActivationFunctionType
ALU = mybir.AluOpType
AX = mybir.AxisListType

SEM_MAX = 30000
D = 1024
NEG = -30000.0


class Tl:
    __slots__ = ("name", "w", "r", "rd", "dram", "ws", "nd", "sem")

    def __init__(self, name, dram=False):
        self.name = name
        self.w = None
        self.r = {}
        self.rd = []
        self.dram = dram
        self.ws = []
        self.nd = 0
        self.sem = None


class Op:
    __slots__ = ("eng", "fn", "deps", "dma", "signal", "sem", "val", "inc", "anchor")

    def __init__(self, eng, fn, dma=False, anchor=None):
        self.eng = eng
        self.fn = fn
        self.deps = []
        self.dma = dma
        self.signal = dma
        self.sem = None
        self.val = 0
        self.inc = 1
        self.anchor = anchor


class Prog:
    ENGS = ("pe", "act", "dve", "pool", "sp")

    def __init__(self, nc):
        self.nc = nc
        self.ops = {e: [] for e in self.ENGS}

    def _dep(self, op, d):
        if d is None or d is op:
            return
        if d.eng == "pe" and op.eng == "pe" and not d.dma and not op.dma:
            return
        op.deps.append(d)

    def op(self, eng, fn, r=(), w=(), dma=False, anchor=None):
        o = Op(eng, fn, dma, anchor)
        for t in r:
            if t.dram:
                for d in t.ws:
                    self._dep(o, d)
            else:
                self._dep(o, t.w)
        for t in w:
            if t.dram:
                continue
            self._dep(o, t.w)
            for d in t.r.values():
                self._dep(o, d)
            for d in t.rd:
                self._dep(o, d)
        for t in r:
            if t.dram:
                continue
            if dma:
                t.rd.append(o)
            else:
                t.r[eng] = o
        for t in w:
            if t.dram:
                t.ws.append(o)
            else:
                t.w = o
                t.r = {}
                t.rd = []
        self.ops[eng].append(o)
        return o

    def barrier(self):
        if not hasattr(self, "bar_idx"):
            self.bar_idx = {e: 0 for e in self.ENGS}
        lasts = []
        for e in self.ENGS:
            for o in reversed(self.ops[e]):
                if not o.dma:
                    lasts.append(o)
                    break
        dmas = [o for e in self.ENGS for o in self.ops[e][self.bar_idx[e]:] if o.dma]
        for e in self.ENGS:
            o = Op(e, lambda eng: eng.nop())
            for d in lasts + dmas:
                self._dep(o, d)
            self.ops[e].append(o)
            self.bar_idx[e] = len(self.ops[e])

    def dma(self, out_ap, in_ap, r, w, anchor, eng="sp"):
        return self.op(eng, lambda e: e.dma_start(out=out_ap, in_=in_ap), r=r, w=w, dma=True, anchor=anchor)

    def emit(self, stack):
        nc = self.nc
        for e in self.ENGS:
            for o in self.ops[e]:
                for d in o.deps:
                    d.signal = True
        engsems = {e: [] for e in self.ENGS}
        semrank = {}
        for e in self.ENGS:
            cnt = SEM_MAX
            cur = None
            for o in self.ops[e]:
                if o.dma:
                    a = o.anchor
                    if a.sem is None:
                        a.sem = stack.enter_context(nc.semaphore("d_" + a.name))
                    a.nd += 1
                    o.sem = a.sem
                    o.val = 16 * a.nd
                    o.inc = 16
                elif o.signal:
                    if cnt >= SEM_MAX:
                        cur = stack.enter_context(nc.semaphore("e_%s_%d" % (e, len(engsems[e]))))
                        semrank[id(cur)] = (e, len(engsems[e]))
                        engsems[e].append(cur)
                        cnt = 0
                    cnt += 1
                    o.sem = cur
                    o.val = cnt
        blk = stack.enter_context(nc.Block())
        engobj = {"pe": "tensor", "act": "scalar", "dve": "vector", "pool": "gpsimd", "sp": "sync"}

        def body(e):
            def run(eng):
                waited = {}
                for o in self.ops[e]:
                    need = {}
                    for d in o.deps:
                        k = id(d.sem)
                        if waited.get(k, 0) >= d.val:
                            continue
                        if k not in need or need[k][1] < d.val:
                            need[k] = (d.sem, d.val)
                    for k, (s, v) in need.items():
                        eng.wait_ge(s, v)
                        waited[k] = v
                        if k in semrank:
                            ee, rk = semrank[k]
                            for j in range(rk):
                                waited[id(engsems[ee][j])] = SEM_MAX + 1
                    ins = o.fn(eng)
                    if o.signal:
                        ins.then_inc(o.sem, o.inc)
            return run

        for e in self.ENGS:
            if self.ops[e]:
                getattr(blk, engobj[e])(body(e))


class Alloc:
    def __init__(self, nc, base=16384, limit=229000):
        self.nc = nc
        self.off = base
        self.limit = limit
        self.n = 0

    def tile(self, shape, dtype, name):
        esz = 2 if dtype == BF16 else 4
        nb = (esz * int(np.prod(shape[1:])) + 63) // 64 * 64
        off = self.off
        self.off += nb
        assert self.off <= self.limit, "SBUF overflow at %s: %d" % (name, self.off)
        self.n += 1
        return self.nc.alloc_sbuf_tensor_at("%s_%d" % (name, self.n), list(shape), dtype, offset=off)

    def mark(self):
        return self.off

    def release(self, m):
        self.off = m


CF = dict(identf=(0, 128), uincl=(128, 256), vaft=(256, 384), cind=(384, 388), invf=(388, 396), eps=(396, 397),
          one=(397, 398), rmask3=(398, 399), vaft128=(400, 528))
CH = 32
NCH = 128 // CH
NCF = 528
CB = dict(ident=(0, 128), caus=(128, 256), m01=(256, 768), onesm=(768, 896))
NCB = 896


def zipper(gens):
    gens = list(gens)
    while gens:
        for g in list(gens):
            try:
                next(g)
            except StopIteration:
                gens.remove(g)


def make_consts():
    cf = np.zeros((128, NCF), np.float32)
    s = np.arange(128)
    ch = s // CH
    same = ch[:, None] == ch[None, :]
    cf[:, 0:128] = np.eye(128)
    cf[:, 128:256] = (same & (s[:, None] <= s[None, :]))
    cf[:, 256:384] = (same & (s[:, None] > s[None, :]))
    for c_ in range(NCH):
        cf[:, 384 + c_] = (ch == c_)
    invf = (np.float32(500000.0) ** (-np.arange(0, 16, 2, dtype=np.float32) / np.float32(16))).astype(np.float32)
    cf[:, 388:396] = invf[None, :]
    cf[:, 396] = 1e-6
    cf[:, 397] = 1.0
    cf[:, 398] = (s >= 96)
    cf[:, 400:528] = (s[:, None] > s[None, :])
    cb = np.zeros((128, NCB), np.float32)
    cb[:, 0:128] = np.eye(128)
    cb[:, 128:256] = np.where(s[:, None] <= s[None, :], 0.0, NEG)
    m01 = (same & (s[:, None] <= s[None, :])).astype(np.float32)
    cb[:, 256:768] = np.tile(m01, (1, 4))
    cb[:, 768:896] = 1.0 / 128.0
    return cf, cb.astype(ml_dtypes.bfloat16)


def build(NBL, dbg=False):
    NOWN = (NBL - 1) // 2
    NTOK = NBL * 128
    NQ = NOWN * 128
    nc = bass.Bass("TRN2", target_bir_lowering=False)

    def din(name, shape, dt=F32):
        return nc.dram_tensor(name, list(shape), dt, kind="ExternalInput").ap()

    skind = "ExternalOutput" if dbg else "Internal"

    def dscr(name, shape, dt):
        return nc.dram_tensor(name, list(shape), dt, kind=skind).ap()

    xl = din("xl", [NTOK, D])
    posl = din("posl", [128, NBL], I32)
    cT = din("cT", [128, 8])
    flag_d = din("flag", [128, 1])
    kb0_d = din("kb0", [128, 1])
    w_ada = din("w_ada", [D, 6 * D])
    b_ada = din("b_ada", [1, 6 * D])
    norm_mix = din("norm_mix", [1, D])
    w_in = din("w_in", [D, 9 * D])
    lamv = din("lamv", [1, 256])
    subw_d = din("subln_w", [128, 1])
    lb_logits = din("lb_logits", [2, D])
    recw_d = din("rec_norm_w", [128, 1])
    w_pa = din("w_proj_att", [D, D])
    w_pr = din("w_proj_rec", [D, D])
    w_o = din("w_out", [D, D])
    norm_mlp = din("norm_mlp", [1, D])
    w_mi = din("w_mlp_in", [D, 4 * D])
    w_mo = din("w_mlp_out", [4 * D, D])
    norm_final = din("norm_final", [1, D])
    cf_d = din("cf", [128, NCF])
    cb_d = din("cb", [128, NCB], BF16)
    out = nc.dram_tensor("out", [NQ, D], F32, kind="ExternalOutput").ap()

    HT = dscr("HT", [NBL, 128, D], BF16)
    KT = dscr("KT", [8, 128, NTOK], BF16)
    Vd = dscr("Vd", [NTOK, D], BF16)
    QT = dscr("QT", [8, 128, NQ], BF16)
    ORT = dscr("ORT", [128, 8, NQ], BF16)
    SGA = dscr("SGA", [128, 8, NQ], BF16)
    SGR = dscr("SGR", [128, 8, NQ], BF16)
    X1 = dscr("X1", [NQ, D], F32)
    ROWS = dscr("ROWS", [8, D], F32)
    T_HT, T_KT, T_V, T_QT, T_ORT, T_SGA, T_SGR, T_X1, T_ROWS = [Tl(n, dram=True) for n in
                                                               ("HT", "KT", "Vd", "QT", "ORT", "SGA", "SGR", "X1", "ROWS")]
    T_out = Tl("out", dram=True)

    A = Alloc(nc)
    P = Prog(nc)
    ntl = [0]

    def T(name):
        ntl[0] += 1
        return Tl("%s%d" % (name, ntl[0]))

    def tile(shape, dt, name):
        return A.tile(shape, dt, name), T(name)

    cf, Tcf = tile([128, NCF], F32, "cf")
    cb, Tcb = tile([128, NCB], BF16, "cb")
    flag, Tflag = tile([128, 1], F32, "flag")
    kb0, Tkb0 = tile([128, 1], F32, "kb0")
    subw, Tsubw = tile([128, 1], F32, "subw")
    recw, Trecw = tile([128, 1], F32, "recw")
    neglam, Tnl = tile([128, 1], F32, "neglam")
    P.dma(cf[:], cf_d, [], [Tcf], Tcf)
    P.dma(cb[:], cb_d, [], [Tcb], Tcb)
    P.dma(flag[:], flag_d, [], [Tflag], Tflag)
    P.dma(kb0[:], kb0_d, [], [Tkb0], Tkb0)
    P.dma(subw[:], subw_d, [], [Tsubw], Tsubw)
    P.dma(recw[:], recw_d, [], [Trecw], Trecw)

    def cfs(k):
        a, b = CF[k]
        return cf[:, a:b]

    def cbs(k):
        a, b = CB[k]
        return cb[:, a:b]

    eps_ap = cfs("eps")
    one_ap = cfs("one")

    PS_all = nc.alloc_psum_tensor("psall", [128, 4096], F32)
    PSB_all = PS_all.bitcast(BF16)
    PS = [PS_all[:, i * 512:(i + 1) * 512] for i in range(8)]
    PSB = [PSB_all[:, i * 1024:(i + 1) * 1024] for i in range(8)]
    bank = [0]
    bank_of = {}
    last_bank = [0]

    def psum(shape, dt, name):
        b = bank[0]
        bank[0] += 1
        assert b < 8, "out of PSUM banks at " + name
        bank_of[name + str(b)] = b
        last_bank[0] = b
        return (PS[b] if dt == F32 else PSB[b]), T(name)

    def end_pass():
        bank[0] = 0
        P.barrier()

    def load_w(dst, Tdst, src, c0, c1, dc0):
        srcv = src.rearrange("(kc p) c -> p kc c", p=128)
        Tdst.dram = True
        for kc in range(dst.shape[1]):
            P.dma(dst[:, kc, dc0:dc0 + (c1 - c0)], srcv[:, kc, c0:c1], [], [Tdst], Tdst, eng="pool")

    mA = A.mark()
    wA, TwA = tile([128, 8, 5120], BF16, "wA")
    load_w(wA, TwA, w_in, 0, 3072, 0)
    load_w(wA, TwA, w_in, 7168, 9216, 3072)
    m0 = A.mark()
    with ExitStack() as pst:
        ps0 = [PS[i] for i in range(6)]
        Tps0 = [T("ps0") for _ in range(6)]
        ct, Tct = tile([128, 8], F32, "ct")
        ce, Tce = tile([128, 8], F32, "ce")
        cond, Tcond = tile([128, 8], F32, "cond")
        P.dma(ct[:], cT, [], [Tct], Tct)
        P.op("act", lambda e: e.activation(out=ce[:], in_=ct[:], func=AF.Exp, scale=-1.0), r=[Tct], w=[Tce])
        P.op("dve", lambda e: e.tensor_scalar(out=ce[:], in0=ce[:], scalar1=1.0, scalar2=None, op0=ALU.add), r=[Tce], w=[Tce])
        P.op("dve", lambda e: e.reciprocal(out=ce[:], in_=ce[:]), r=[Tce], w=[Tce])
        P.op("dve", lambda e: e.tensor_tensor(out=cond[:], in0=ct[:], in1=ce[:], op=ALU.mult), r=[Tct, Tce], w=[Tcond])
        rowt, Trow = tile([1, 8 * D], F32, "rowt")
        adar, Tadar = tile([1, 6 * D], F32, "adar")
        badar, Tbadar = tile([1, 6 * D], F32, "badar")
        nrm, Tnrm = tile([1, 2 * D], F32, "nrm")
        lbt, Tlbt = tile([1, 2 * D], F32, "lbt")
        lamt, Tlamt = tile([1, 256], F32, "lamt")
        lsm, Tlsm = tile([1, 8], F32, "lsm")
        P.dma(badar[:], b_ada, [], [Tbadar], Tbadar)
        P.dma(nrm[:, 0:D], norm_mix, [], [Tnrm], Tnrm)
        P.dma(nrm[:, D:2 * D], norm_mlp, [], [Tnrm], Tnrm)
        P.dma(lbt[:, 0:D], lb_logits[0:1, :], [], [Tlbt], Tlbt)
        P.dma(lbt[:, D:2 * D], lb_logits[1:2, :], [], [Tlbt], Tlbt)
        P.dma(lamt[:], lamv, [], [Tlamt], Tlamt)
        wst = [tile([128, 3072], F32, "wst") for _ in range(2)]
        it = 0
        for half in range(2):
            for kc in range(8):
                w_t, Tw = wst[it % 2]
                it += 1
                P.dma(w_t[:], w_ada[kc * 128:(kc + 1) * 128, half * 3072:(half + 1) * 3072], [], [Tw], Tw)
                for g in range(6):
                    P.op("pe", lambda e, g=g, kc=kc, w_t=w_t: e.matmul(out=ps0[g][0:1, :], lhsT=cond[:, kc:kc + 1],
                                                                       rhs=w_t[:, g * 512:(g + 1) * 512],
                                                                       start=(kc == 0), stop=(kc == 7)),
                         r=[Tcond, Tw], w=[Tps0[g]])
            for g in range(6):
                c0 = half * 3072 + g * 512
                P.op("dve", lambda e, g=g, c0=c0: e.tensor_tensor(out=adar[:, c0:c0 + 512], in0=ps0[g][0:1, :],
                                                                   in1=badar[:, c0:c0 + 512], op=ALU.add),
                     r=[Tps0[g], Tbadar], w=[Tadar])
        def ad(i):
            return adar[:, i * D:(i + 1) * D]

        def row(i):
            return rowt[:, i * D:(i + 1) * D]
        P.op("dve", lambda e: e.scalar_tensor_tensor(out=row(0), in0=ad(1), scalar=1.0, in1=nrm[:, 0:D], op0=ALU.add, op1=ALU.mult),
             r=[Tadar, Tnrm], w=[Trow])
        P.op("dve", lambda e: e.tensor_copy(out=row(1), in_=ad(0)), r=[Tadar], w=[Trow])
        P.op("dve", lambda e: e.tensor_copy(out=row(2), in_=ad(2)), r=[Tadar], w=[Trow])
        P.op("dve", lambda e: e.scalar_tensor_tensor(out=row(3), in0=ad(4), scalar=1.0, in1=nrm[:, D:2 * D], op0=ALU.add, op1=ALU.mult),
             r=[Tadar, Tnrm], w=[Trow])
        P.op("dve", lambda e: e.tensor_copy(out=row(4), in_=ad(3)), r=[Tadar], w=[Trow])
        P.op("dve", lambda e: e.tensor_copy(out=row(5), in_=ad(5)), r=[Tadar], w=[Trow])
        P.op("dve", lambda e: e.tensor_tensor(out=lbt[:, 0:D], in0=lbt[:, 0:D], in1=lbt[:, D:2 * D], op=ALU.subtract), r=[Tlbt], w=[Tlbt])
        P.op("act", lambda e: e.activation(out=lbt[:, 0:D], in_=lbt[:, 0:D], func=AF.Exp), r=[Tlbt], w=[Tlbt])
        P.op("dve", lambda e: e.tensor_scalar(out=lbt[:, 0:D], in0=lbt[:, 0:D], scalar1=1.0, scalar2=None, op0=ALU.add), r=[Tlbt], w=[Tlbt])
        P.op("dve", lambda e: e.reciprocal(out=row(6), in_=lbt[:, 0:D]), r=[Tlbt], w=[Trow])
        P.op("dve", lambda e: e.tensor_tensor(out=lamt[:, 0:64], in0=lamt[:, 0:64], in1=lamt[:, 64:128], op=ALU.mult), r=[Tlamt], w=[Tlamt])
        P.op("dve", lambda e: e.tensor_tensor(out=lamt[:, 128:192], in0=lamt[:, 128:192], in1=lamt[:, 192:256], op=ALU.mult), r=[Tlamt], w=[Tlamt])
        P.op("dve", lambda e: e.reduce_sum(out=lsm[:, 0:1], in_=lamt[:, 0:64], axis=AX.X), r=[Tlamt], w=[Tlsm])
        P.op("dve", lambda e: e.reduce_sum(out=lsm[:, 1:2], in_=lamt[:, 128:192], axis=AX.X), r=[Tlamt, Tlsm], w=[Tlsm])
        P.op("act", lambda e: e.activation(out=lsm[:, 2:4], in_=lsm[:, 0:2], func=AF.Exp), r=[Tlsm], w=[Tlsm])
        P.op("dve", lambda e: e.tensor_tensor(out=lsm[:, 4:5], in0=lsm[:, 3:4], in1=lsm[:, 2:3], op=ALU.subtract), r=[Tlsm], w=[Tlsm])
        P.op("dve", lambda e: e.memset(row(7), 0.0), r=[], w=[Trow])
        P.op("dve", lambda e: e.tensor_scalar(out=rowt[:, 7 * D:7 * D + 1], in0=lsm[:, 4:5], scalar1=-0.2, scalar2=None, op0=ALU.add),
             r=[Tlsm, Trow], w=[Trow])
        P.dma(ROWS.rearrange("(o r) d -> o (r d)", o=1), rowt[:], [Trow], [T_ROWS], Trow)
        P.dma(neglam[:], ROWS[7:8, 0:1].partition_broadcast(128)[:, 0, :], [T_ROWS], [Tnl], Tnl)
    A.release(m0)
    end_pass()

    def bload(ap_t, T_t, src_row):
        P.dma(ap_t[:], src_row.partition_broadcast(128)[:, 0, :], [T_ROWS], [T_t], T_t)

    def rmsnorm_affine(xt, Tx, ssq, Tssq, junk, Tjunk, tmp, Ttmp, Av, TA, Bv, TB, hout, Thout):
        P.op("dve", lambda e: e.memset(ssq[:, 0:1], 0.0), r=[], w=[Tssq])
        P.op("act", lambda e: e.activation(out=junk[:], in_=xt[:], func=AF.Square, scale=1.0 / 32.0, accum_out=ssq[:, 0:1]),
             r=[Tx, Tssq], w=[Tjunk, Tssq])
        P.op("act", lambda e: e.activation(out=ssq[:, 1:2], in_=ssq[:, 0:1], func=AF.Ln, bias=eps_ap, scale=1.0), r=[Tssq, Tcf], w=[Tssq])
        P.op("act", lambda e: e.activation(out=ssq[:, 2:3], in_=ssq[:, 1:2], func=AF.Exp, scale=-0.5), r=[Tssq], w=[Tssq])
        P.op("dve", lambda e: e.scalar_tensor_tensor(out=tmp[:], in0=xt[:], scalar=ssq[:, 2:3], in1=Av[:], op0=ALU.mult, op1=ALU.mult),
             r=[Tx, Tssq, TA], w=[Ttmp])
        if Bv is not None:
            P.op("dve", lambda e: e.tensor_tensor(out=hout[:], in0=tmp[:], in1=Bv[:], op=ALU.add), r=[Ttmp, TB], w=[Thout])

    with ExitStack() as pst:
        Am, TAm = tile([128, D], F32, "Am")
        Bm, TBm = tile([128, D], F32, "Bm")
        bload(Am, TAm, ROWS[0:1, :])
        bload(Bm, TBm, ROWS[1:2, :])
        pit, Tpit = tile([128, NBL], I32, "pit")
        pft, Tpft = tile([128, NBL], F32, "pft")
        ang, Tang = tile([128, NBL, 8], F32, "ang")
        angc, Tangc = tile([128, NBL, 8], F32, "angc")
        kf, Tkf = tile([128, NBL, 8], F32, "kf")
        ki, Tki = tile([128, NBL, 8], I32, "ki")
        mk, Tmk = tile([128, NBL, 8], F32, "mk")
        cost, Tcos = tile([128, NBL, 8], F32, "cost")
        sint, Tsin = tile([128, NBL, 8], F32, "sint")
        P.dma(pit[:], posl, [], [Tpit], Tpit)
        P.op("dve", lambda e: e.tensor_copy(out=pft[:], in_=pit[:]), r=[Tpit], w=[Tpft])
        P.op("dve", lambda e: e.tensor_tensor(out=ang[:], in0=pft[:].unsqueeze(2).to_broadcast([128, NBL, 8]),
                                              in1=cfs("invf").unsqueeze(1).to_broadcast([128, NBL, 8]), op=ALU.mult),
             r=[Tpft, Tcf], w=[Tang])
        P.op("dve", lambda e: e.tensor_scalar(out=angc[:], in0=ang[:], scalar1=float(np.pi / 2), scalar2=None, op0=ALU.add), r=[Tang], w=[Tangc])
        C1 = 6.28125
        C2 = float(2 * np.pi - 6.28125)

        def sin_of(a_t, Ta, o_t, To):
            P.op("dve", lambda e: e.tensor_scalar(out=kf[:], in0=a_t[:], scalar1=float(1 / (2 * np.pi)), scalar2=None, op0=ALU.mult), r=[Ta], w=[Tkf])
            P.op("dve", lambda e: e.tensor_copy(out=ki[:], in_=kf[:]), r=[Tkf], w=[Tki])
            P.op("dve", lambda e: e.tensor_copy(out=kf[:], in_=ki[:]), r=[Tki], w=[Tkf])
            P.op("dve", lambda e: e.scalar_tensor_tensor(out=a_t[:], in0=kf[:], scalar=-C1, in1=a_t[:], op0=ALU.mult, op1=ALU.add), r=[Tkf, Ta], w=[Ta])
            P.op("dve", lambda e: e.scalar_tensor_tensor(out=a_t[:], in0=kf[:], scalar=-C2, in1=a_t[:], op0=ALU.mult, op1=ALU.add), r=[Tkf, Ta], w=[Ta])
            P.op("dve", lambda e: e.tensor_single_scalar(out=mk[:], in_=a_t[:], scalar=float(np.pi), op=ALU.is_gt), r=[Ta], w=[Tmk])
            P.op("dve", lambda e: e.scalar_tensor_tensor(out=a_t[:], in0=mk[:], scalar=float(-2 * np.pi), in1=a_t[:], op0=ALU.mult, op1=ALU.add), r=[Tmk, Ta], w=[Ta])
            P.op("dve", lambda e: e.tensor_scalar(out=a_t[:], in0=a_t[:], scalar1=float(-np.pi), scalar2=float(np.pi), op0=ALU.max, op1=ALU.min), r=[Ta], w=[Ta])
            P.op("act", lambda e: e.activation(out=o_t[:], in_=a_t[:], func=AF.Sin), r=[Ta], w=[To])
        sin_of(ang, Tang, sint, Tsin)
        sin_of(angc, Tangc, cost, Tcos)

        psT, TpsT = psum([128, 1024], BF16, "psT")
        psKT, TpsKT = psum([128, 1024], BF16, "psKT")
        psP = [psum([128, 512], F32, "psP") for _ in range(4)]
        psG = [psum([128, 512], F32, "psG") for _ in range(2)]
        xs = [tile([128, D], F32, "xs") for _ in range(2)]
        ssqs = [tile([128, 4], F32, "ssq") for _ in range(2)]
        junks = [tile([128, D], BF16, "junk") for _ in range(2)]
        tmps = [tile([128, D], F32, "tmp") for _ in range(2)]
        hbs = [tile([128, D], BF16, "hb") for _ in range(2)]
        hT = [tile([128, D], BF16, "hT") for _ in range(2)]
        ksbs = [tile([128, D], BF16, "ksb") for _ in range(2)]
        qsb, Tqsb = tile([128, D], BF16, "qsb")
        kTs = [tile([128, D], BF16, "kTs") for _ in range(2)]
        qTs = [tile([128, D], BF16, "qTs") for _ in range(2)]
        vsb = [tile([128, D], BF16, "vsb") for _ in range(2)]
        egt, Tegt = tile([128, D], F32, "egt")
        sgs = [tile([128, D], BF16, "sgs") for _ in range(4)]
        rr = [tile([128, 16, 8], F32, "rr") for _ in range(4)]
        pidx = [0]

        def proj_tok(hTt, ThT, wt, Tw, c0, evac):
            for g in range(2):
                ps, Tps = psP[pidx[0] % len(psP)]
                pidx[0] += 1
                for kc in range(8):
                    P.op("pe", lambda e, kc=kc, g=g, ps=ps: e.matmul(out=ps[:], lhsT=hTt[:, kc * 128:(kc + 1) * 128],
                                                                       rhs=wt[:, kc, c0 + g * 512:c0 + (g + 1) * 512],
                                                                       start=(kc == 0), stop=(kc == 7)),
                         r=[ThT, Tw], w=[Tps])
                evac(g, ps, Tps)

        def rope(t, Tt, m):
            tv = t[:].rearrange("p (g d) -> p g d", g=16)
            cs = cost[:, m, :].unsqueeze(1).to_broadcast([128, 16, 8])
            sn = sint[:, m, :].unsqueeze(1).to_broadcast([128, 16, 8])
            t1 = tv[:, :, 0:8]
            t2 = tv[:, :, 8:16]
            (ra, Tra), (rb, Trb), (rc, Trc), (rd, Trd) = rr
            P.op("dve", lambda e: e.tensor_tensor(out=ra[:], in0=t1, in1=cs, op=ALU.mult), r=[Tt, Tcos], w=[Tra])
            P.op("dve", lambda e: e.tensor_tensor(out=rb[:], in0=t2, in1=sn, op=ALU.mult), r=[Tt, Tsin], w=[Trb])
            P.op("dve", lambda e: e.tensor_tensor(out=rc[:], in0=t2, in1=cs, op=ALU.mult), r=[Tt, Tcos], w=[Trc])
            P.op("dve", lambda e: e.tensor_tensor(out=rd[:], in0=t1, in1=sn, op=ALU.mult), r=[Tt, Tsin], w=[Trd])
            P.op("dve", lambda e: e.tensor_tensor(out=t1, in0=ra[:], in1=rb[:], op=ALU.subtract), r=[Tra, Trb, Tt], w=[Tt])
            P.op("dve", lambda e: e.tensor_tensor(out=t2, in0=rc[:], in1=rd[:], op=ALU.add), r=[Trc, Trd, Tt], w=[Tt])

        def transpose8(src, Tsrc, ps, Tps, dst, Tdst, eng):
            for h in range(8):
                P.op("pe", lambda e, h=h: e.transpose(out=ps[:, h * 128:(h + 1) * 128], in_=src[:, h * 128:(h + 1) * 128], identity=cbs("ident")),
                     r=[Tsrc, Tcb], w=[Tps])
            if eng == "act":
                P.op("act", lambda e: e.activation(out=dst[:], in_=ps[:], func=AF.Copy), r=[Tps], w=[Tdst])
            else:
                P.op(eng, lambda e: e.tensor_copy(out=dst[:], in_=ps[:]), r=[Tps], w=[Tdst])

        def copy_evac(dst, Tdst, eng="act"):
            def f(g, ps, Tps):
                if eng == "act":
                    P.op("act", lambda e: e.activation(out=dst[:, g * 512:(g + 1) * 512], in_=ps[:], func=AF.Copy), r=[Tps], w=[Tdst])
                else:
                    P.op(eng, lambda e: e.tensor_copy(out=dst[:, g * 512:(g + 1) * 512], in_=ps[:]), r=[Tps], w=[Tdst])
            return f

        KTv = KT.rearrange("h p n -> p h n")
        QTv = QT.rearrange("h p n -> p h n")
        def pa_block(m, phase):
            s2 = m % 2
            own = (m % 2 == 1)
            i = (m - 1) // 2
            x_t, Tx = xs[s2]
            ssq, Tssq = ssqs[s2]
            junk, Tjunk = junks[s2]
            tmp, Ttmp = tmps[s2]
            hb, Thb = hbs[s2]
            ksb, Tksb = ksbs[s2]
            hTt, ThT = hT[s2]
            if phase == 0:
                P.dma(x_t[:], xl[m * 128:(m + 1) * 128, :], [], [Tx], Tx)
                yield
                rmsnorm_affine(x_t, Tx, ssq, Tssq, junk, Tjunk, tmp, Ttmp, Am, TAm, Bm, TBm, hb, Thb)
                yield
                if m == 0:
                    P.op("pool", lambda e: e.tensor_scalar(out=hb[:], in0=hb[:], scalar1=flag[:, 0:1], scalar2=None, op0=ALU.mult), r=[Thb, Tflag], w=[Thb])
                    yield
                transpose8(hb, Thb, psT, TpsT, hTt, ThT, "act")
                yield
                P.dma(HT[m], hTt[:], [ThT], [T_HT], ThT, eng="pool")
                yield
                return
            proj_tok(hTt, ThT, wA, TwA, 1024, copy_evac(ksb, Tksb, "act"))
            yield
            rope(ksb, Tksb, m)
            yield
            kT_t, TkT = kTs[s2]
            transpose8(ksb, Tksb, psKT, TpsKT, kT_t, TkT, "dve")
            yield
            P.dma(KTv[:, :, m * 128:(m + 1) * 128], kT_t[:].rearrange("p (h n) -> p h n", h=8), [TkT], [T_KT], TkT, eng="pool")
            yield
            v_t, Tv = vsb[s2]
            proj_tok(hTt, ThT, wA, TwA, 2048, copy_evac(v_t, Tv, "dve"))
            yield
            P.dma(Vd[m * 128:(m + 1) * 128, :], v_t[:], [Tv], [T_V], Tv, eng="pool")
            yield
            if own:
                proj_tok(hTt, ThT, wA, TwA, 0, copy_evac(qsb, Tqsb, "act"))
                yield
                rope(qsb, Tqsb, m)
                yield
                qT_t, TqT = qTs[i % 2]
                transpose8(qsb, Tqsb, psKT, TpsKT, qT_t, TqT, "dve")
                yield
                P.dma(QTv[:, :, i * 128:(i + 1) * 128], qT_t[:].rearrange("p (h n) -> p h n", h=8), [TqT], [T_QT], TqT, eng="pool")
                yield
                for gi, (c0, DST, TDST) in enumerate(((3072, SGA, T_SGA), (4096, SGR, T_SGR))):
                    for oc in range(8):
                        ps, Tps = psG[oc // 4]
                        for kc in range(8):
                            P.op("pe", lambda e, oc=oc, kc=kc, ps=ps, c0=c0, hTt=hTt: e.matmul(
                                out=ps[:, (oc % 4) * 128:(oc % 4 + 1) * 128], lhsT=wA[:, kc, c0 + oc * 128:c0 + (oc + 1) * 128],
                                rhs=hTt[:, kc * 128:(kc + 1) * 128], start=(kc == 0), stop=(kc == 7)), r=[TwA, ThT], w=[Tps])
                            yield
                    for g in range(2):
                        ps, Tps = psG[g]
                        P.op("act", lambda e, g=g, ps=ps: e.activation(out=egt[:, g * 512:(g + 1) * 512], in_=ps[:], func=AF.Exp, scale=-1.0),
                             r=[Tps], w=[Tegt])
                        yield
                    P.op("act", lambda e: e.activation(out=egt[:], in_=egt[:], func=AF.Ln, bias=one_ap, scale=1.0), r=[Tegt, Tcf], w=[Tegt])
                    yield
                    sg_t, Tsg = sgs[(2 * i + gi) % 4]
                    P.op("act", lambda e, sg_t=sg_t: e.activation(out=sg_t[:], in_=egt[:], func=AF.Exp, scale=-1.0), r=[Tegt], w=[Tsg])
                    yield
                    P.dma(DST[:, :, i * 128:(i + 1) * 128], sg_t[:].rearrange("p (h n) -> p h n", h=8), [Tsg], [TDST], Tsg, eng="pool")
                    yield
        zipper([pa_block(0, 0)])
        for m in range(NBL):
            zipper(([pa_block(m + 1, 0)] if m + 1 < NBL else []) + [pa_block(m, 1)])
    A.release(mA)
    end_pass()

    with ExitStack() as pst:
        wH, TwH = tile([128, 8, 4096], BF16, "wH")
        load_w(wH, TwH, w_in, 3072, 7168, 0)
        LB, TLB = tile([128, D], F32, "LB")
        bload(LB, TLB, ROWS[6:7, :])
        psP1, TpsP1 = psum([128, 512], F32, "psP1")
        psC, TpsC = psum([128, 16], F32, "psC")
        psX = [psum([128, 512], F32, "psX") for _ in range(2)]
        psQ = [psum([128, 512], F32, "psQ") for _ in range(2)]
        psO = [psum([128, 512], F32, "psO") for _ in range(2)]
        hT = [tile([128, D], BF16, "hT") for _ in range(2)]
        risb = [tile([128, D], BF16, "risb") for _ in range(2)]
        Es = [tile([128, D], F32, "E") for _ in range(2)]
        Ls = [tile([128, D], F32, "L") for _ in range(2)]
        Tts = [tile([128, D], F32, "Tt") for _ in range(2)]
        khats = [tile([128, D], BF16, "khat") for _ in range(2)]
        ebCs = [tile([128, 32], F32, "ebC") for _ in range(2)]
        khcs = [[tile([128, D], BF16, "khc") for _ in range(NCH)] for _ in range(2)]
        Ss = [(A.tile([128, D], F32, "S"), [T("S") for _ in range(8)]) for _ in range(2)]
        cur = [0]
        Sb = [tile([128, D], BF16, "Sb") for _ in range(NCH)]
        epb, Tepb = tile([128, D], F32, "epb")
        enb, Tenb = tile([128, D], F32, "enb")
        kt, Tkt = tile([128, D], BF16, "kt")
        qt, Tqt = tile([128, D], BF16, "qt")
        scm, Tscm = tile([128, D], BF16, "scm")
        sq, Tsq = tile([128, D], BF16, "sq")
        rstdT, TrstdT = tile([128, D], F32, "rstdT")
        gate, Tgate = tile([128, D], F32, "gate")
        on, Ton = tile([128, D], F32, "on")
        ors = [tile([128, D], BF16, "ors") for _ in range(2)]
        P.op("pool", lambda e: e.memset(Ss[0][0][:], 0.0), r=[], w=Ss[0][1])

        def proj_tok1(hTt, ThT, c0, evac):
            for g in range(2):
                for kc in range(8):
                    P.op("pe", lambda e, kc=kc, g=g: e.matmul(out=psP1[:], lhsT=hTt[:, kc * 128:(kc + 1) * 128],
                                                               rhs=wH[:, kc, c0 + g * 512:c0 + (g + 1) * 512],
                                                               start=(kc == 0), stop=(kc == 7)), r=[ThT, TwH], w=[TpsP1])
                evac(g)

        def proj_feat(hTt, ThT, c0, pss):
            for h in range(8):
                ps, Tps = pss[h // 4]
                for kc in range(8):
                    P.op("pe", lambda e, h=h, kc=kc, ps=ps: e.matmul(out=ps[:, (h % 4) * 128:(h % 4 + 1) * 128],
                                                                       lhsT=wH[:, kc, c0 + h * 128:c0 + (h + 1) * 128],
                                                                       rhs=hTt[:, kc * 128:(kc + 1) * 128], start=(kc == 0), stop=(kc == 7)),
                         r=[TwH, ThT], w=[Tps])

        def ph_block(m, phase):
            s2 = m % 2
            own = (m % 2 == 1)
            i = (m - 1) // 2
            E, TE = Es[s2]
            L, TL = Ls[s2]
            Tt, TTt = Tts[s2]
            khat, Tkhat = khats[s2]
            ebC, TebC = ebCs[s2]
            khc = khcs[s2]
            hTt, ThT = hT[s2]
            ri_t, Tri = risb[s2]
            if phase == 0:
                P.dma(hTt[:], HT[m], [T_HT], [ThT], ThT)
                yield
                proj_tok1(hTt, ThT, 1024, lambda g: P.op("act", lambda e: e.activation(out=E[:, g * 512:(g + 1) * 512], in_=psP1[:], func=AF.Exp, scale=-1.0),
                                                         r=[TpsP1], w=[TE]))
                yield
                proj_tok1(hTt, ThT, 2048, lambda g, ri_t=ri_t, Tri=Tri: P.op("dve", lambda e: e.tensor_copy(out=ri_t[:, g * 512:(g + 1) * 512], in_=psP1[:]),
                                                         r=[TpsP1], w=[Tri]))
                yield
                P.op("act", lambda e: e.activation(out=L[:], in_=E[:], func=AF.Ln, bias=one_ap, scale=1.0), r=[TE, Tcf], w=[TL])
                yield
                P.op("dve", lambda e: e.tensor_tensor(out=Tt[:], in0=E[:], in1=LB[:], op=ALU.mult), r=[TE, TLB], w=[TTt])
                yield
                P.op("act", lambda e: e.activation(out=Tt[:], in_=Tt[:], func=AF.Ln, bias=one_ap, scale=1.0), r=[TTt, Tcf], w=[TTt])
                yield
                P.op("dve", lambda e: e.tensor_tensor(out=Tt[:], in0=Tt[:], in1=L[:], op=ALU.subtract), r=[TTt, TL], w=[TTt])
                yield
                P.op("act", lambda e: e.activation(out=E[:], in_=Tt[:], func=AF.Exp), r=[TTt], w=[TE])
                yield
                P.op("dve", lambda e: e.tensor_scalar(out=E[:], in0=E[:], scalar1=-1.0, scalar2=1.0, op0=ALU.mult, op1=ALU.add), r=[TE], w=[TE])
                yield
                for g in range(2):
                    P.op("pe", lambda e, g=g: e.matmul(out=psP1[:], lhsT=cfs("vaft" if own else "vaft128"), rhs=Tt[:, g * 512:(g + 1) * 512], start=True, stop=True),
                         r=[Tcf, TTt], w=[TpsP1])
                    yield
                    P.op("act", lambda e, g=g: e.activation(out=L[:, g * 512:(g + 1) * 512], in_=psP1[:], func=AF.Exp), r=[TpsP1], w=[TL])
                    yield
                P.op("dve", lambda e: e.tensor_tensor(out=khat[:], in0=E[:], in1=L[:], op=ALU.mult), r=[TE, TL], w=[Tkhat])
                yield
                if own:
                    for c in range(NCH):
                        P.op("dve", lambda e, c=c: e.tensor_scalar(out=khc[c][0][:], in0=khat[:], scalar1=cf[:, 384 + c:385 + c], scalar2=None, op0=ALU.mult),
                             r=[Tkhat, Tcf], w=[khc[c][1]])
                        yield
                    for h in range(8):
                        P.op("pe", lambda e, h=h: e.matmul(out=psC[:, NCH * h:NCH * h + NCH], lhsT=Tt[:, h * 128:(h + 1) * 128], rhs=cfs("cind"), start=True, stop=True),
                             r=[TTt, Tcf], w=[TpsC])
                        yield
                    P.op("act", lambda e: e.activation(out=ebC[:], in_=psC[:, 0:32], func=AF.Exp), r=[TpsC], w=[TebC])
                    yield
                else:
                    for h in range(8):
                        P.op("pe", lambda e, h=h: e.matmul(out=psC[:, h:h + 1], lhsT=Tt[:, h * 128:(h + 1) * 128], rhs=one_ap, start=True, stop=True),
                             r=[TTt, Tcf], w=[TpsC])
                        yield
                    P.op("act", lambda e: e.activation(out=ebC[:, 0:8], in_=psC[:, 0:8], func=AF.Exp), r=[TpsC], w=[TebC])
                    yield
                return
            if own:
                S0, TS0 = Ss[cur[0]]
                P.op("act", lambda e, S0=S0: e.activation(out=Sb[0][0][:], in_=S0[:], func=AF.Copy), r=TS0, w=[Sb[0][1]])
                yield
            for c in range(NCH if own else 1):
                kx, Tkx = khc[c] if own else (khat, Tkhat)
                for h in range(8):
                    ps, Tps = psX[h // 4]
                    P.op("pe", lambda e, h=h, ps=ps, ri_t=ri_t, kx=kx: e.matmul(out=ps[:, (h % 4) * 128:(h % 4 + 1) * 128],
                                                                     lhsT=kx[:, h * 128:(h + 1) * 128],
                                                                     rhs=ri_t[:, h * 128:(h + 1) * 128], start=True, stop=True),
                         r=[Tkx, Tri], w=[Tps])
                    yield
                Sa, TSa = Ss[cur[0]]
                Sn, TSn = Ss[1 - cur[0]]
                cur[0] = 1 - cur[0]
                for h in range(8):
                    ps, Tps = psX[h // 4]
                    col = (NCH * h + c) if own else h
                    P.op("dve", lambda e, h=h, ps=ps, Sa=Sa, Sn=Sn, col=col: e.scalar_tensor_tensor(out=Sn[:, h * 128:(h + 1) * 128], in0=Sa[:, h * 128:(h + 1) * 128],
                                                                                    scalar=ebC[:, col:col + 1],
                                                                                    in1=ps[:, (h % 4) * 128:(h % 4 + 1) * 128], op0=ALU.mult, op1=ALU.add),
                         r=[TSa[h], TebC, Tps], w=[TSn[h]])
                    yield
                if own and c < NCH - 1:
                    P.op("act", lambda e, c=c, Sn=Sn: e.activation(out=Sb[c + 1][0][:], in_=Sn[:], func=AF.Copy), r=TSn, w=[Sb[c + 1][1]])
                    yield
            if not own:
                return
            for h in range(8):
                ps, Tps = psX[h // 4]
                P.op("pe", lambda e, h=h, ps=ps: e.matmul(out=ps[:, (h % 4) * 128:(h % 4 + 1) * 128], lhsT=Tt[:, h * 128:(h + 1) * 128],
                                                            rhs=cfs("uincl"), start=True, stop=True), r=[TTt, Tcf], w=[Tps])
                yield
            for g in range(2):
                ps, Tps = psX[g]
                P.op("act", lambda e, g=g, ps=ps: e.activation(out=epb[:, g * 512:(g + 1) * 512], in_=ps[:], func=AF.Exp), r=[Tps], w=[Tepb])
                yield
                P.op("act", lambda e, g=g, ps=ps: e.activation(out=enb[:, g * 512:(g + 1) * 512], in_=ps[:], func=AF.Exp, scale=-1.0), r=[Tps], w=[Tenb])
                yield
            for h in range(8):
                ps, Tps = psX[h // 4]
                P.op("pe", lambda e, h=h, ps=ps: e.transpose(out=ps[:, (h % 4) * 128:(h % 4 + 1) * 128], in_=E[:, h * 128:(h + 1) * 128],
                                                               identity=cfs("identf")), r=[TE, Tcf], w=[Tps])
                yield
            for g in range(2):
                ps, Tps = psX[g]
                P.op("dve", lambda e, g=g, ps=ps: e.tensor_tensor(out=kt[:, g * 512:(g + 1) * 512], in0=ps[:], in1=enb[:, g * 512:(g + 1) * 512], op=ALU.mult),
                     r=[Tps, Tenb], w=[Tkt])
                yield
            proj_feat(hTt, ThT, 0, psQ)
            yield
            for g in range(2):
                ps, Tps = psQ[g]
                P.op("dve", lambda e, g=g, ps=ps: e.tensor_tensor(out=qt[:, g * 512:(g + 1) * 512], in0=ps[:], in1=epb[:, g * 512:(g + 1) * 512], op=ALU.mult),
                     r=[Tps, Tepb], w=[Tqt])
                yield
            for h in range(8):
                ps, Tps = psX[h // 4]
                P.op("pe", lambda e, h=h, ps=ps: e.matmul(out=ps[:, (h % 4) * 128:(h % 4 + 1) * 128], lhsT=kt[:, h * 128:(h + 1) * 128],
                                                            rhs=qt[:, h * 128:(h + 1) * 128], start=True, stop=True), r=[Tkt, Tqt], w=[Tps])
                yield
            for g in range(2):
                ps, Tps = psX[g]
                P.op("dve", lambda e, g=g, ps=ps: e.tensor_tensor(out=scm[:, g * 512:(g + 1) * 512], in0=ps[:], in1=cbs("m01"), op=ALU.mult),
                     r=[Tps, Tcb], w=[Tscm])
                yield
            for h in range(8):
                ps, Tps = psO[h // 4]
                for c in range(NCH):
                    c0 = (h % 4) * 128 + c * CH
                    P.op("pe", lambda e, h=h, c=c, ps=ps, c0=c0, ri_t=ri_t: e.matmul(out=ps[:, c0:c0 + CH], lhsT=ri_t[:, h * 128:(h + 1) * 128],
                                                                            rhs=scm[:, h * 128 + c * CH:h * 128 + (c + 1) * CH], start=True, stop=False),
                         r=[Tri, Tscm], w=[Tps])
                    yield
                    P.op("pe", lambda e, h=h, c=c, ps=ps, c0=c0: e.matmul(out=ps[:, c0:c0 + CH], lhsT=Sb[c][0][:, h * 128:(h + 1) * 128],
                                                                            rhs=qt[:, h * 128 + c * CH:h * 128 + (c + 1) * CH], start=False, stop=True),
                         r=[Sb[c][1], Tqt], w=[Tps])
                    yield
            for g in range(2):
                ps, Tps = psO[g]
                P.op("act", lambda e, g=g, ps=ps: e.activation(out=sq[:, g * 512:(g + 1) * 512], in_=ps[:], func=AF.Square), r=[Tps], w=[Tsq])
                yield
            for g in range(2):
                ps, Tps = psX[g]
                P.op("pe", lambda e, g=g, ps=ps: e.matmul(out=ps[:], lhsT=cbs("onesm"), rhs=sq[:, g * 512:(g + 1) * 512], start=True, stop=True),
                     r=[Tcb, Tsq], w=[Tps])
                yield
                P.op("act", lambda e, g=g, ps=ps: e.activation(out=rstdT[:, g * 512:(g + 1) * 512], in_=ps[:], func=AF.Ln, bias=eps_ap, scale=1.0),
                     r=[Tps, Tcf], w=[TrstdT])
                yield
            P.op("act", lambda e: e.activation(out=rstdT[:], in_=rstdT[:], func=AF.Exp, scale=-0.5), r=[TrstdT], w=[TrstdT])
            yield
            proj_feat(hTt, ThT, 3072, psQ)
            yield
            for g in range(2):
                ps, Tps = psQ[g]
                P.op("act", lambda e, g=g, ps=ps: e.activation(out=gate[:, g * 512:(g + 1) * 512], in_=ps[:], func=AF.Exp, scale=-1.0), r=[Tps], w=[Tgate])
                yield
            P.op("act", lambda e: e.activation(out=gate[:], in_=gate[:], func=AF.Ln, bias=one_ap, scale=1.0), r=[Tgate, Tcf], w=[Tgate])
            yield
            P.op("act", lambda e: e.activation(out=gate[:], in_=gate[:], func=AF.Exp, scale=-1.0), r=[Tgate], w=[Tgate])
            yield
            for g in range(2):
                ps, Tps = psQ[g]
                P.op("dve", lambda e, g=g, ps=ps: e.tensor_tensor(out=gate[:, g * 512:(g + 1) * 512], in0=ps[:], in1=gate[:, g * 512:(g + 1) * 512], op=ALU.mult),
                     r=[Tps, Tgate], w=[Tgate])
                yield
            for g in range(2):
                ps, Tps = psO[g]
                P.op("dve", lambda e, g=g, ps=ps: e.scalar_tensor_tensor(out=on[:, g * 512:(g + 1) * 512], in0=ps[:], scalar=recw[:, 0:1],
                                                                           in1=rstdT[:, g * 512:(g + 1) * 512], op0=ALU.mult, op1=ALU.mult),
                     r=[Tps, Trecw, TrstdT], w=[Ton])
                yield
            or_t, Tor = ors[i % 2]
            P.op("dve", lambda e, or_t=or_t: e.tensor_tensor(out=or_t[:], in0=on[:], in1=gate[:], op=ALU.mult), r=[Ton, Tgate], w=[Tor])
            yield
            P.dma(ORT[:, :, i * 128:(i + 1) * 128], or_t[:].rearrange("p (h n) -> p h n", h=8), [Tor], [T_ORT], Tor, eng="pool")
            yield
        zipper([ph_block(0, 0)])
        for m in range(NBL):
            zipper(([ph_block(m + 1, 0)] if m + 1 < NBL else []) + [ph_block(m, 1)])
    A.release(mA)
    end_pass()

    OAT, TOAT = tile([128, 8, NQ], BF16, "OAT")
    mB = A.mark()
    with ExitStack() as pst:
        psS2 = []
        psS2b = []
        for _ in range(2):
            pr_ = [psum([128, 512], F32, "psS2") for _ in range(2)]
            psS2.append(pr_)
            psS2b.append(last_bank[0] - 1)
        psO = [psum([128, 512], F32, "psOa") for _ in range(2)]
        psD = [psum([128, 512], F32, "psDa") for _ in range(2)]
        kth = [tile([128, NTOK], BF16, "kth") for _ in range(2)]
        vth = [tile([128, NBL, 128], BF16, "vth") for _ in range(2)]
        qth = [tile([128, NQ], BF16, "qth") for _ in range(2)]
        PT = [tile([128, 2, 512], BF16, "PT") for _ in range(3)]
        onesb, Tonesb = tile([128, 128], BF16, "onesb")
        subw8, Tsubw8 = tile([128, 1], F32, "subw8")
        rr1, Trr1 = tile([128, 512], F32, "rr1")
        rr2, Trr2 = tile([128, 512], F32, "rr2")
        tt1, Ttt1 = tile([128, 512], F32, "tt1")
        oa, Toa = tile([128, 512], F32, "oa")
        sqb, Tsqb = tile([128, 512], BF16, "sqb")
        rsd, Trsd = tile([128, 512], F32, "rsd")
        P.op("dve", lambda e: e.memset(onesb[:], 1.0), r=[], w=[Tonesb])
        P.op("dve", lambda e: e.tensor_scalar(out=subw8[:], in0=subw[:], scalar1=0.8, scalar2=None, op0=ALU.mult), r=[Tsubw], w=[Tsubw8])
        Vv = Vd.rearrange("(m p) c -> p m c", p=128)
        NG = (NOWN + 3) // 4
        LOOK = 1
        units = []

        def load_head(h):
            k_t, Tk = kth[h % 2]
            v_t, Tv = vth[h % 2]
            q_t, Tq = qth[h % 2]
            P.dma(k_t[:], KT[h], [T_KT], [Tk], Tk)
            for mb in range(0, NBL, 13):
                me = min(NBL, mb + 13)
                P.dma(v_t[:, mb:me, :], Vv[:, mb:me, h * 128:(h + 1) * 128], [T_V], [Tv], Tv)
            P.dma(q_t[:], QT[h], [T_QT], [Tq], Tq)

        def make_unit(h, i0, nj, kb, nkb, uidx, pre):
            k_t, Tk = kth[h % 2]
            v_t, Tv = vth[h % 2]
            q_t, Tq = qth[h % 2]
            pss = psS2[uidx % 2]
            pt, Tpt = PT[uidx % 3]
            jmin = max(0, kb // 2 - i0)
            diag = (kb % 2 == 1) and ((kb - 1) // 2 - i0 >= 0)
            qbase = i0 * 128
            c0, c1 = jmin * 128, nj * 128

            def qk():
                for mp in range(2):
                    ps, Tps = pss[mp]
                    kl = k_t[mp * 64:(mp + 1) * 64, kb * 128:(kb + 1) * 128]

                    def qsl(a, b, mp=mp):
                        return q_t[mp * 64:(mp + 1) * 64, qbase + a * 128:qbase + b * 128]
                    if diag:
                        jd = jmin
                        P.op("pe", lambda e, ps=ps, kl=kl, qs=qsl(jd, jd + 1): e.matmul(out=ps[:, jd * 128:(jd + 1) * 128], lhsT=kl, rhs=qs, start=True, stop=False),
                             r=[Tk, Tq], w=[Tps])
                        P.op("pe", lambda e, ps=ps: e.matmul(out=ps[:, jd * 128:(jd + 1) * 128], lhsT=cbs("ident"), rhs=cbs("caus"), start=False, stop=True),
                             r=[Tcb], w=[Tps])
                        if jd + 1 < nj:
                            P.op("pe", lambda e, ps=ps, kl=kl, qs=qsl(jd + 1, nj): e.matmul(out=ps[:, (jd + 1) * 128:nj * 128], lhsT=kl, rhs=qs, start=True, stop=True),
                                 r=[Tk, Tq], w=[Tps])
                    else:
                        P.op("pe", lambda e, ps=ps, kl=kl, qs=qsl(jmin, nj): e.matmul(out=ps[:, c0:c1], lhsT=kl, rhs=qs, start=True, stop=True),
                             r=[Tk, Tq], w=[Tps])

            def ex():
                b0 = psS2b[uidx % 2]
                src = PS_all[:, b0 * 512:(b0 + 2) * 512].rearrange("p (m n) -> p m n", m=2)[:, :, c0:c1]
                Tp0, Tp1 = pss[0][1], pss[1][1]
                if kb == 0:
                    P.op("act", lambda e: e.activation(out=pt[:, :, c0:c1], in_=src, func=AF.Exp, bias=kb0[:, 0:1], scale=0.125),
                         r=[Tp0, Tp1, Tkb0], w=[Tpt])
                else:
                    P.op("act", lambda e: e.activation(out=pt[:, :, c0:c1], in_=src, func=AF.Exp, scale=0.125),
                         r=[Tp0, Tp1], w=[Tpt])

            def pv():
                if pre is not None:
                    pre()
                for mp in range(2):
                    po, Tpo = psO[mp]
                    pd, Tpd = psD[mp]
                    P.op("pe", lambda e, po=po, mp=mp: e.matmul(out=po[:, c0:c1], lhsT=v_t[:, kb, :], rhs=pt[:, mp, c0:c1], start=(kb == 0), stop=(kb == nkb - 1)),
                         r=[Tv, Tpt], w=[Tpo])
                    P.op("pe", lambda e, pd=pd, mp=mp: e.matmul(out=pd[:, c0:c1], lhsT=onesb[:], rhs=pt[:, mp, c0:c1], start=(kb == 0), stop=(kb == nkb - 1)),
                         r=[Tonesb, Tpt], w=[Tpd])
            return qk, ex, pv

        def make_post(h, i0, nj):
            w = nj * 128

            def post():
                (po1, Tpo1), (po2, Tpo2) = psO
                (pd1, Tpd1), (pd2, Tpd2) = psD
                P.op("act", lambda e: e.activation(out=rr1[:, 0:w], in_=pd1[:, 0:w], func=AF.Ln), r=[Tpd1], w=[Trr1])
                P.op("act", lambda e: e.activation(out=rr1[:, 0:w], in_=rr1[:, 0:w], func=AF.Exp, scale=-1.0), r=[Trr1], w=[Trr1])
                P.op("act", lambda e: e.activation(out=rr2[:, 0:w], in_=pd2[:, 0:w], func=AF.Ln), r=[Tpd2], w=[Trr2])
                P.op("act", lambda e: e.activation(out=rr2[:, 0:w], in_=rr2[:, 0:w], func=AF.Exp, scale=-1.0), r=[Trr2], w=[Trr2])
                P.op("dve", lambda e: e.tensor_tensor(out=tt1[:, 0:w], in0=po1[:, 0:w], in1=rr1[:, 0:w], op=ALU.mult), r=[Tpo1, Trr1], w=[Ttt1])
                P.op("dve", lambda e: e.tensor_tensor(out=rr2[:, 0:w], in0=po2[:, 0:w], in1=rr2[:, 0:w], op=ALU.mult), r=[Tpo2, Trr2], w=[Trr2])
                P.op("dve", lambda e: e.scalar_tensor_tensor(out=oa[:, 0:w], in0=rr2[:, 0:w], scalar=neglam[:, 0:1], in1=tt1[:, 0:w], op0=ALU.mult, op1=ALU.add),
                     r=[Trr2, Tnl, Ttt1], w=[Toa])
                P.op("act", lambda e: e.activation(out=sqb[:, 0:w], in_=oa[:, 0:w], func=AF.Square), r=[Toa], w=[Tsqb])
                P.op("pe", lambda e: e.matmul(out=pd1[:, 0:w], lhsT=cbs("onesm"), rhs=sqb[:, 0:w], start=True, stop=True), r=[Tcb, Tsqb], w=[Tpd1])
                P.op("act", lambda e: e.activation(out=rsd[:, 0:w], in_=pd1[:, 0:w], func=AF.Ln, bias=eps_ap, scale=1.0), r=[Tpd1, Tcf], w=[Trsd])
                P.op("act", lambda e: e.activation(out=rsd[:, 0:w], in_=rsd[:, 0:w], func=AF.Exp, scale=-0.5), r=[Trsd], w=[Trsd])
                P.op("dve", lambda e: e.scalar_tensor_tensor(out=OAT[:, h, i0 * 128:i0 * 128 + w], in0=oa[:, 0:w], scalar=subw8[:, 0:1], in1=rsd[:, 0:w],
                                                             op0=ALU.mult, op1=ALU.mult), r=[Toa, Tsubw8, Trsd], w=[TOAT])
            return post

        uidx = 0
        for h in range(8):
            first = True
            for G in range(NG):
                i0 = 4 * G
                nj = min(4, NOWN - i0)
                nkb = 2 * (i0 + nj - 1) + 2
                for kb in range(nkb):
                    pre = None
                    if first:
                        first = False
                        if h >= 1 and h + 1 < 8:
                            pre = (lambda h=h: load_head(h + 1))
                    qk, ex, pv = make_unit(h, i0, nj, kb, nkb, uidx, pre)
                    post = make_post(h, i0, nj) if kb == nkb - 1 else None
                    units.append((qk, ex, pv, post))
                    uidx += 1
        NU = len(units)
        load_head(0)
        load_head(1)
        for t in range(NU + LOOK):
            if t < NU:
                units[t][0]()
                units[t][1]()
            if t - LOOK >= 0:
                u = units[t - LOOK]
                u[2]()
                if u[3] is not None:
                    u[3]()
    if dbg:
        OATd = nc.dram_tensor("OATd", [128, 8, NQ], BF16, kind="ExternalOutput").ap()
        P.dma(OATd, OAT[:], [TOAT], [T_out], TOAT)
    A.release(mB)
    end_pass()

    with ExitStack() as pst:
        wP, TwP = tile([128, 8, 3 * D], BF16, "wP")
        load_w(wP, TwP, w_pa, 0, D, 0)
        load_w(wP, TwP, w_pr, 0, D, D)
        load_w(wP, TwP, w_o, 0, D, 2 * D)
        Gm, TGm = tile([128, D], F32, "Gm")
        bload(Gm, TGm, ROWS[2:3, :])
        psA = [psum([128, 256], F32, "psA") for _ in range(2)]
        psR = [psum([128, 256], F32, "psR") for _ in range(2)]
        psZ = [psum([128, 512], F32, "psZ") for _ in range(2)]
        NT = 256
        ort = [tile([128, 8, NT], BF16, "ort") for _ in range(2)]
        sga = [tile([128, 8, NT], BF16, "sga") for _ in range(2)]
        sgr = [tile([128, 8, NT], BF16, "sgr") for _ in range(2)]
        y1, Ty1 = tile([128, NT], F32, "y1")
        y2, Ty2 = tile([128, NT], F32, "y2")
        yT, TyT = tile([128, 8, NT], BF16, "yT")
        xs = [tile([128, D], F32, "xs") for _ in range(2)]
        zt, Tzt = tile([128, D], F32, "zt")
        x1s = [tile([128, D], F32, "x1s") for _ in range(2)]
        ngrp = (NQ + NT - 1) // NT
        bi = 0
        for gq in range(ngrp):
            q0 = gq * NT
            nt = min(NT, NQ - q0)
            o_t, To = ort[gq % 2]
            a_t, Ta = sga[gq % 2]
            r_t, Tr = sgr[gq % 2]
            P.dma(o_t[:, :, 0:nt], ORT[:, :, q0:q0 + nt], [T_ORT], [To], To)
            P.dma(a_t[:, :, 0:nt], SGA[:, :, q0:q0 + nt], [T_SGA], [Ta], Ta)
            P.dma(r_t[:, :, 0:nt], SGR[:, :, q0:q0 + nt], [T_SGR], [Tr], Tr)
            for oc in range(8):
                pa, Tpa = psA[oc % 2]
                pr, Tpr = psR[oc % 2]
                for hh in range(8):
                    P.op("pe", lambda e, oc=oc, hh=hh, pa=pa, q0=q0, nt=nt: e.matmul(out=pa[:, 0:nt], lhsT=wP[:, hh, oc * 128:(oc + 1) * 128],
                                                                                       rhs=OAT[:, hh, q0:q0 + nt], start=(hh == 0), stop=(hh == 7)),
                         r=[TwP, TOAT], w=[Tpa])
                for hh in range(8):
                    P.op("pe", lambda e, oc=oc, hh=hh, pr=pr, o_t=o_t, nt=nt: e.matmul(out=pr[:, 0:nt], lhsT=wP[:, hh, D + oc * 128:D + (oc + 1) * 128],
                                                                                         rhs=o_t[:, hh, 0:nt], start=(hh == 0), stop=(hh == 7)),
                         r=[TwP, To], w=[Tpr])
                P.op("dve", lambda e, oc=oc, pa=pa, a_t=a_t, nt=nt: e.tensor_tensor(out=y1[:, 0:nt], in0=pa[:, 0:nt], in1=a_t[:, oc, 0:nt], op=ALU.mult),
                     r=[Tpa, Ta], w=[Ty1])
                P.op("dve", lambda e, oc=oc, pr=pr, r_t=r_t, nt=nt: e.tensor_tensor(out=y2[:, 0:nt], in0=pr[:, 0:nt], in1=r_t[:, oc, 0:nt], op=ALU.mult),
                     r=[Tpr, Tr], w=[Ty2])
                P.op("dve", lambda e, oc=oc, nt=nt: e.tensor_tensor(out=yT[:, oc, 0:nt], in0=y1[:, 0:nt], in1=y2[:, 0:nt], op=ALU.add),
                     r=[Ty1, Ty2], w=[TyT])
            for jb in range(nt // 128):
                i = (q0 // 128) + jb
                m = 2 * i + 1
                x_t, Tx = xs[bi % 2]
                x1_t, Tx1 = x1s[bi % 2]
                bi += 1
                P.dma(x_t[:], xl[m * 128:(m + 1) * 128, :], [], [Tx], Tx)
                for g in range(2):
                    pz, Tpz = psZ[g]
                    for oc in range(8):
                        P.op("pe", lambda e, g=g, oc=oc, pz=pz, jb=jb: e.matmul(out=pz[:], lhsT=yT[:, oc, jb * 128:(jb + 1) * 128],
                                                                                  rhs=wP[:, oc, 2 * D + g * 512:2 * D + (g + 1) * 512],
                                                                                  start=(oc == 0), stop=(oc == 7)), r=[TyT, TwP], w=[Tpz])
                    P.op("dve", lambda e, g=g, pz=pz: e.tensor_tensor(out=zt[:, g * 512:(g + 1) * 512], in0=pz[:], in1=Gm[:, g * 512:(g + 1) * 512], op=ALU.mult),
                         r=[Tpz, TGm], w=[Tzt])
                P.op("dve", lambda e, x_t=x_t, x1_t=x1_t: e.tensor_tensor(out=x1_t[:], in0=zt[:], in1=x_t[:], op=ALU.add), r=[Tzt, Tx], w=[Tx1])
                P.dma(X1[i * 128:(i + 1) * 128, :], x1_t[:], [Tx1], [T_X1], Tx1, eng="pool")
    A.release(mA)
    end_pass()

    with ExitStack() as pst:
        wI, TwI = tile([128, 8, 4 * D], BF16, "wI")
        wO, TwO = tile([128, 32, D], BF16, "wO")
        load_w(wI, TwI, w_mi, 0, 2048, 0)
        load_w(wI, TwI, w_mi, 2048, 4096, 2048)
        wov = w_mo.rearrange("(fc p) c -> p fc c", p=128)
        TwO.dram = True
        for fc in range(32):
            P.dma(wO[:, fc, :], wov[:, fc, :], [], [TwO], TwO, eng="pool")
        Af, TAf = tile([128, D], F32, "Af")
        Bf, TBf = tile([128, D], F32, "Bf")
        Gf, TGf = tile([128, D], F32, "Gf")
        NF, TNF = tile([128, D], F32, "NF")
        bload(Af, TAf, ROWS[3:4, :])
        bload(Bf, TBf, ROWS[4:5, :])
        bload(Gf, TGf, ROWS[5:6, :])
        P.dma(NF[:], norm_final.partition_broadcast(128)[:, 0, :], [], [TNF], TNF)
        psT2, TpsT2 = psum([128, 1024], BF16, "psT2")
        psU = [psum([128, 512], F32, "psU") for _ in range(3)]
        psM = [psum([128, 512], F32, "psM") for _ in range(4)]
        NT = 256
        x1t = [tile([128, D], F32, "x1t") for _ in range(3)]
        ssq, Tssq = tile([128, 8], F32, "ssq2")
        junk, Tjunk = tile([128, D], BF16, "junk2")
        tmp, Ttmp = tile([128, D], F32, "tmp2")
        h2, Th2 = tile([128, D], BF16, "h2")
        h2T, Th2T = tile([128, 8, NT], BF16, "h2T")
        rl = [tile([128, 512], BF16, "rl") for _ in range(2)]
        uT, TuT = tile([128, 32, NT], BF16, "uT")
        x2, Tx2 = tile([128, D], F32, "x2")
        ot = [tile([128, D], F32, "ot") for _ in range(2)]
        ngrp = (NQ + NT - 1) // NT
        xi = 0
        ui = 0
        mi = 0
        oi = 0
        for gq in range(ngrp):
            q0 = gq * NT
            nt = min(NT, NQ - q0)
            nb = nt // 128
            xt_l = []
            for jb in range(nb):
                x_t, Tx = x1t[xi % 3]
                xi += 1
                xt_l.append((x_t, Tx))
                i = q0 // 128 + jb
                P.dma(x_t[:], X1[i * 128:(i + 1) * 128, :], [T_X1], [Tx], Tx)
                rmsnorm_affine(x_t, Tx, ssq, Tssq, junk, Tjunk, tmp, Ttmp, Af, TAf, Bf, TBf, h2, Th2)
                for kc in range(8):
                    P.op("pe", lambda e, kc=kc: e.transpose(out=psT2[:, kc * 128:(kc + 1) * 128], in_=h2[:, kc * 128:(kc + 1) * 128], identity=cbs("ident")),
                         r=[Th2, Tcb], w=[TpsT2])
                P.op("act", lambda e, jb=jb: e.activation(out=h2T[:, :, jb * 128:(jb + 1) * 128], in_=psT2[:].rearrange("p (k n) -> p k n", k=8), func=AF.Copy),
                     r=[TpsT2], w=[Th2T])
            for f2 in range(16):
                pu, Tpu = psU[ui % 3]
                r_t, Trl = rl[ui % 2]
                ui += 1
                for ff in range(2):
                    fc = 2 * f2 + ff
                    for kc in range(8):
                        P.op("pe", lambda e, fc=fc, ff=ff, kc=kc, pu=pu, nt=nt: e.matmul(out=pu[:, ff * 256:ff * 256 + nt], lhsT=wI[:, kc, fc * 128:(fc + 1) * 128],
                                                                                           rhs=h2T[:, kc, 0:nt], start=(kc == 0), stop=(kc == 7)),
                             r=[TwI, Th2T], w=[Tpu])
                P.op("act", lambda e, pu=pu, r_t=r_t: e.activation(out=r_t[:], in_=pu[:], func=AF.Relu), r=[Tpu], w=[Trl])
                P.op("dve", lambda e, f2=f2, r_t=r_t: e.tensor_tensor(out=uT[:, 2 * f2:2 * f2 + 2, :], in0=r_t[:].rearrange("p (f n) -> p f n", f=2),
                                                                         in1=r_t[:].rearrange("p (f n) -> p f n", f=2), op=ALU.mult),
                     r=[Trl], w=[TuT])
            for jb in range(nb):
                x_t, Tx = xt_l[jb]
                i = q0 // 128 + jb
                pms = []
                for g in range(2):
                    pm, Tpm = psM[mi % 4]
                    mi += 1
                    pms.append((pm, Tpm))
                    for fc in range(32):
                        P.op("pe", lambda e, g=g, fc=fc, pm=pm, jb=jb: e.matmul(out=pm[:], lhsT=uT[:, fc, jb * 128:(jb + 1) * 128],
                                                                                  rhs=wO[:, fc, g * 512:(g + 1) * 512], start=(fc == 0), stop=(fc == 31)),
                             r=[TuT, TwO], w=[Tpm])
                for g in range(2):
                    pm, Tpm = pms[g]
                    P.op("dve", lambda e, g=g, pm=pm: e.tensor_tensor(out=tmp[:, g * 512:(g + 1) * 512], in0=pm[:], in1=Gf[:, g * 512:(g + 1) * 512], op=ALU.mult),
                         r=[Tpm, TGf], w=[Ttmp])
                P.op("dve", lambda e, x_t=x_t: e.tensor_tensor(out=x2[:], in0=tmp[:], in1=x_t[:], op=ALU.add), r=[Ttmp, Tx], w=[Tx2])
                o_t, Tot = ot[oi % 2]
                oi += 1
                rmsnorm_affine(x2, Tx2, ssq, Tssq, junk, Tjunk, o_t, Tot, NF, TNF, None, None, None, None)
                P.dma(out[i * 128:(i + 1) * 128, :], o_t[:], [Tot], [T_out], Tot, eng="pool")
    fin = Tl("fin")
    P.op("sp", lambda e: e.nop(), r=[T_out], w=[])
    st = ExitStack()
    st.enter_context(nc.allow_low_precision("bf16 operands, fp32 accumulation"))
    st.enter_context(nc.allow_non_contiguous_dma("strided scratch layouts"))
    P.emit(st)
    st.close()
    return nc


def shard_inputs(inp, NBL):
    cf, cb = make_consts()
    x = np.asarray(inp["x"], np.float32)
    B = x.shape[0]
    pos = np.asarray(inp["positions"], np.int32)
    S = x.shape[1]
    NTOK = NBL * 128
    lamv = np.concatenate([np.asarray(inp[k], np.float32).reshape(1, 64) for k in ("lam_q1", "lam_k1", "lam_q2", "lam_k2")], axis=1)
    common = {
        "w_ada": np.ascontiguousarray(inp["w_ada"][0], np.float32), "b_ada": np.ascontiguousarray(inp["b_ada"], np.float32).reshape(1, -1),
        "norm_mix": np.asarray(inp["norm_mix"], np.float32).reshape(1, -1), "w_in": np.ascontiguousarray(inp["w_in"][0], np.float32),
        "lamv": np.ascontiguousarray(lamv), "subln_w": np.asarray(inp["subln_w"], np.float32).reshape(128, 1),
        "lb_logits": np.ascontiguousarray(inp["lb_logits"], np.float32), "rec_norm_w": np.asarray(inp["rec_norm_w"], np.float32).reshape(128, 1),
        "w_proj_att": np.ascontiguousarray(inp["w_proj_att"][0], np.float32), "w_proj_rec": np.ascontiguousarray(inp["w_proj_rec"][0], np.float32),
        "w_out": np.ascontiguousarray(inp["w_out"][0], np.float32), "norm_mlp": np.asarray(inp["norm_mlp"], np.float32).reshape(1, -1),
        "w_mlp_in": np.ascontiguousarray(inp["w_mlp_in"][0], np.float32), "w_mlp_out": np.ascontiguousarray(inp["w_mlp_out"][0], np.float32),
        "norm_final": np.asarray(inp["norm_final"], np.float32).reshape(1, -1), "cf": cf, "cb": cb,
    }
    maps = []
    for b in range(B):
        for p in range(2):
            start = 128 * (p - 1)
            xl = np.zeros((NTOK, D), np.float32)
            pl = np.zeros((NTOK,), np.int32)
            lo = max(0, start)
            hi = min(S, start + NTOK)
            xl[lo - start:hi - start] = x[b, lo:hi]
            pl[lo - start:hi - start] = pos[b, lo:hi]
            d = dict(common)
            d["xl"] = xl
            d["posl"] = np.ascontiguousarray(pl.reshape(NBL, 128).T)
            d["cT"] = np.ascontiguousarray(np.asarray(inp["c"], np.float32)[b].reshape(8, 128).T)
            d["flag"] = np.full((128, 1), float(p), np.float32)
            d["kb0"] = np.full((128, 1), 0.0 if p == 1 else NEG, np.float32)
            maps.append(d)
    return maps


def gather(results, B, NBL):
    NOWN = (NBL - 1) // 2
    S = 2 * NOWN * 128
    out = np.zeros((B, S, D), np.float32)
    for b in range(B):
        for p in range(2):
            o = np.asarray(results[2 * b + p]["out"]).reshape(NOWN, 128, D)
            ov = out[b].reshape(NOWN, 2, 128, D)
            ov[:, p] = o
    return out


_NC_CACHE = {}


def kernel(**inputs):
    NBL = 65
    if NBL not in _NC_CACHE:
        _NC_CACHE[NBL] = build(NBL)
    nc = _NC_CACHE[NBL]
    maps = shard_inputs(inputs, NBL)
    res = run_bass_kernel_spmd(nc, maps, core_ids=list(range(8)))
    return gather(res.results, 4, NBL)
```
